# Optimizing a Trainium2 kernel written in Bass

```python
import math
import jax, jax.numpy as jnp
from jax import lax
import numpy as np

D_MODEL = 1024
BATCH = 4
SEQ = 8192
DEPTH = 4

CONV_WIDTH = 512
CONV_KSIZE = 3
ATTN_HEADS = 8
HEAD_DIM = 64
ATTN_WIDTH = ATTN_HEADS * HEAD_DIM
IDX_HEADS = 8
IDX_DIM = 64
INDEX_TOPK = 256
Q_BLOCK = 128
ROPE_THETA = 10000.0
POOL_WINDOWS = (2, 4, 8, 16)
POOL_GROUPS = 4
POOL_WIDTH = 512
POOL_GROUP_DIM = POOL_WIDTH // POOL_GROUPS
N_BRANCHES = 3
RMS_EPS = 1e-6

IN_SPLITS = (CONV_WIDTH, CONV_WIDTH, CONV_WIDTH, CONV_WIDTH,
             ATTN_WIDTH, ATTN_WIDTH, ATTN_WIDTH, ATTN_WIDTH,
             IDX_HEADS * IDX_DIM, IDX_DIM, IDX_HEADS,
             POOL_WIDTH, POOL_WIDTH,
             N_BRANCHES * D_MODEL)
IN_WIDTH = 4 * CONV_WIDTH + 4 * ATTN_WIDTH + IDX_HEADS * IDX_DIM + IDX_DIM + IDX_HEADS + 2 * POOL_WIDTH + N_BRANCHES * D_MODEL

kernel_name = "hybrid_gated_conv_dsa_pool_trunk"


def rms_norm(x, g):
    xf = x.astype(jnp.float32)
    y = xf * lax.rsqrt(jnp.mean(xf * xf, axis=-1, keepdims=True) + RMS_EPS)
    return (y * g.astype(jnp.float32)).astype(x.dtype)


def split_columns(p):
    out = []
    off = 0
    for n in IN_SPLITS:
        out.append(p[..., off:off + n])
        off += n
    return out


def rope_tables(seq_len, dim):
    inv = 1.0 / (ROPE_THETA ** (jnp.arange(0, dim, 2, dtype=jnp.float32) / dim))
    ang = jnp.arange(seq_len, dtype=jnp.float32)[:, None] * inv[None, :]
    return jnp.cos(ang), jnp.sin(ang)


def apply_rope(x, cos, sin):
    xf = x.astype(jnp.float32)
    half = xf.shape[-1] // 2
    x1, x2 = xf[..., :half], xf[..., half:]
    c, s = cos[None, :, None, :], sin[None, :, None, :]
    return jnp.concatenate([x1 * c - x2 * s, x2 * c + x1 * s], axis=-1).astype(x.dtype)


def short_conv_mixer(b, c, xin, gate, conv_w, w_out):
    S = xin.shape[1]
    z = c * xin
    zp = jnp.pad(z, ((0, 0), (CONV_KSIZE - 1, 0), (0, 0)))
    y = conv_w[0] * zp[:, 0:S]
    for j in range(1, CONV_KSIZE):
        y = y + conv_w[j] * zp[:, j:j + S]
    y = b * y * jax.nn.silu(gate)
    return y @ w_out


def sparse_attention_mixer(q, k, v, gate, iq, ik, iw, q_g, k_g, cos, sin, w_out):
    B, S, _ = q.shape
    q = apply_rope(rms_norm(q.reshape(B, S, ATTN_HEADS, HEAD_DIM), q_g), cos, sin)
    k = apply_rope(rms_norm(k.reshape(B, S, ATTN_HEADS, HEAD_DIM), k_g), cos, sin)
    v = v.reshape(B, S, ATTN_HEADS, HEAD_DIM)
    iq = apply_rope(iq.reshape(B, S, IDX_HEADS, IDX_DIM), cos, sin)
    ik = apply_rope(ik[:, :, None, :], cos, sin)[:, :, 0, :]
    top_k = min(INDEX_TOPK, S // 4)
    nb = S // Q_BLOCK

    def to_blocks(a):
        return jnp.swapaxes(a.reshape((B, nb, Q_BLOCK) + a.shape[2:]), 0, 1)

    starts = jnp.arange(nb, dtype=jnp.int32) * Q_BLOCK
    key_pos = jnp.arange(S, dtype=jnp.int32)

    def block_fn(args):
        qb, iqb, iwb, t0 = args
        tq = t0 + jnp.arange(Q_BLOCK, dtype=jnp.int32)
        sc = jnp.einsum('bqhd,bsd->bqhs', iqb, ik).astype(jnp.float32) * (IDX_DIM ** -0.5)
        score = jnp.einsum('bqhs,bqh->bqs', jax.nn.relu(sc), iwb.astype(jnp.float32)) * (IDX_HEADS ** -0.5)
        admissible = key_pos[None, :] <= tq[:, None]
        score = jnp.where(admissible[None], score, -jnp.inf)
        _, idx = lax.top_k(score, top_k)
        valid = idx <= tq[None, :, None]
        k_sel = jax.vmap(lambda kb, ib: kb[ib])(k, idx)
        v_sel = jax.vmap(lambda vb, ib: vb[ib])(v, idx)
        s = jnp.einsum('bqhd,bqkhd->bhqk', qb, k_sel).astype(jnp.float32) * (HEAD_DIM ** -0.5)
        s = jnp.where(valid[:, None], s, -jnp.inf)
        p = jax.nn.softmax(s, axis=-1).astype(v.dtype)
        return jnp.einsum('bhqk,bqkhd->bqhd', p, v_sel)

    o = lax.map(block_fn, (to_blocks(q), to_blocks(iq), to_blocks(iw), starts))
    o = jnp.swapaxes(o, 0, 1).reshape(B, S, ATTN_WIDTH)
    return (o * jax.nn.silu(gate)) @ w_out


def pool_mixer(u, gate, pool_w, pool_scale, w_out):
    B, S, _ = u.shape
    ug = u.reshape(B, S, POOL_GROUPS, POOL_GROUP_DIM)
    cs = jnp.cumsum(ug.astype(jnp.float32), axis=1)
    t = jnp.arange(S)
    means = []
    for gi, w in enumerate(POOL_WINDOWS):
        c = cs[:, :, gi]
        lagged = jnp.pad(c, ((0, 0), (w, 0), (0, 0)))[:, :S]
        cnt = jnp.minimum(t + 1, w).astype(jnp.float32)[None, :, None]
        means.append((c - lagged) / cnt)
    pooled = (jnp.stack(means, axis=2) - ug.astype(jnp.float32)).astype(u.dtype)
    mixed = jnp.einsum('bsgc,gcd->bsgd', pooled, pool_w).reshape(B, S, POOL_WIDTH)
    y = mixed * pool_scale * jax.nn.silu(gate)
    return y @ w_out


def setup_inputs(seed: int = 0) -> dict:
    key = jax.random.key(seed)
    ks = jax.random.split(key, 13)
    f32 = jnp.float32
    L = DEPTH
    return {
        "x": jax.random.normal(ks[0], (BATCH, SEQ, D_MODEL), f32),
        "norm_g": 1.0 + 0.1 * jax.random.normal(ks[1], (L, D_MODEL), f32),
        "w_in": jax.random.normal(ks[2], (L, D_MODEL, IN_WIDTH), f32) * D_MODEL ** -0.5,
        "conv_w": jax.random.normal(ks[3], (L, CONV_KSIZE, CONV_WIDTH), f32) * CONV_KSIZE ** -0.5,
        "w_out_conv": jax.random.normal(ks[4], (L, CONV_WIDTH, D_MODEL), f32) * CONV_WIDTH ** -0.5,
        "q_norm_g": 1.0 + 0.1 * jax.random.normal(ks[5], (L, HEAD_DIM), f32),
        "k_norm_g": 1.0 + 0.1 * jax.random.normal(ks[6], (L, HEAD_DIM), f32),
        "w_out_attn": jax.random.normal(ks[7], (L, ATTN_WIDTH, D_MODEL), f32) * ATTN_WIDTH ** -0.5,
        "pool_w": jax.random.normal(ks[8], (L, POOL_GROUPS, POOL_GROUP_DIM, POOL_GROUP_DIM), f32) * POOL_GROUP_DIM ** -0.5,
        "pool_scale": 1.0 + 0.1 * jax.random.normal(ks[9], (L, POOL_WIDTH), f32),
        "w_out_pool": jax.random.normal(ks[10], (L, POOL_WIDTH, D_MODEL), f32) * POOL_WIDTH ** -0.5,
        "w_o": jax.random.normal(ks[11], (L, D_MODEL, D_MODEL), f32) * D_MODEL ** -0.5,
    }


def reference(x, norm_g, w_in, conv_w, w_out_conv, q_norm_g, k_norm_g, w_out_attn, pool_w, pool_scale, w_out_pool, w_o):
    B, S, D = x.shape
    cos, sin = rope_tables(S, HEAD_DIM)
    for l in range(DEPTH):
        h = rms_norm(x, norm_g[l])
        p = h @ w_in[l]
        (cb, cc, cx, cgate, q, k, v, agate, iq, ik, iw, pu, pgate, mgate) = split_columns(p)
        y_a = short_conv_mixer(cb, cc, cx, cgate, conv_w[l], w_out_conv[l])
        y_b = sparse_attention_mixer(q, k, v, agate, iq, ik, iw, q_norm_g[l], k_norm_g[l], cos, sin, w_out_attn[l])
        y_c = pool_mixer(pu, pgate, pool_w[l], pool_scale[l], w_out_pool[l])
        g = jax.nn.sigmoid(mgate.reshape(B, S, N_BRANCHES, D))
        merged = g[:, :, 0] * y_a + g[:, :, 1] * y_b + g[:, :, 2] * y_c
        x = x + merged @ w_o[l]
    return x
```

```python
import math
from contextlib import ExitStack

import numpy as np
import ml_dtypes

import concourse.bass as bass
import concourse.mybir as mybir
from concourse.bass_utils import run_bass_kernel_spmd

F32 = mybir.dt.float32
BF16 = mybir.dt.bfloat16
ALU = mybir.AluOpType
AF = mybir.ActivationFunctionType
AX = mybir.AxisListType

PE, ACT, DVE, POOL, SP = "pe", "act", "dve", "pool", "sp"
COMPUTE = (PE, ACT, DVE, POOL)
N_LANES = 32

D = 1024
DEPTH = 4
SEQ = 8192
BATCH = 4
INW = 8776
C_CB, C_CC, C_CX, C_CG = 0, 512, 1024, 1536
C_Q, C_K, C_V, C_AG = 2048, 2560, 3072, 3584
C_IQ, C_IK, C_IW = 4096, 4608, 4672
C_PU, C_PG = 4680, 5192
C_MG = 5704
TOPK = 256
RMS_EPS = 1e-6
POOL_WINDOWS = (2, 4, 8, 16)
NITER = 20
NEG = -1.0e30


class Buf:
    __slots__ = ("name", "last_w", "readers")

    def __init__(self, name):
        self.name = name
        self.last_w = None
        self.readers = {}


class Sched:
    def __init__(self):
        self.ops = {e: [] for e in (PE, ACT, DVE, POOL, SP)}
        self.cnt = {e: 0 for e in COMPUTE}
        self.waited = {e: {} for e in (PE, ACT, DVE, POOL, SP)}
        self.lane_cnt = [0] * N_LANES
        self.next_lane = 0
        self.n_ops = 0

    def _deps(self, reads, writes):
        deps = {}
        reads = [b for b in reads if b.name != "dram"]
        writes = [b for b in writes if b.name != "dram"]
        for b in reads:
            t = b.last_w
            if t is not None and deps.get(t[0], 0) < t[1]:
                deps[t[0]] = t[1]
        for b in writes:
            t = b.last_w
            if t is not None and deps.get(t[0], 0) < t[1]:
                deps[t[0]] = t[1]
            for k, v in b.readers.items():
                if deps.get(k, 0) < v:
                    deps[k] = v
        return deps

    def _commit(self, tok, reads, writes):
        k, v = tok
        reads = [b for b in reads if b.name != "dram"]
        writes = [b for b in writes if b.name != "dram"]
        for b in writes:
            b.last_w = tok
            b.readers = {}
        for b in reads:
            if b.readers.get(k, 0) < v:
                b.readers[k] = v

    def _filter(self, eng, deps):
        waits = []
        wd = self.waited[eng]
        for k, v in deps.items():
            if k == eng:
                if eng == PE:
                    continue
                if v <= self.cnt[eng] - 3:
                    continue
            if wd.get(k, 0) >= v:
                continue
            wd[k] = v
            waits.append((k, v))
        return waits

    def op(self, eng, fn, reads=(), writes=()):
        deps = self._deps(reads, writes)
        waits = self._filter(eng, deps)
        self.cnt[eng] += 1
        tok = (eng, self.cnt[eng])
        self.ops[eng].append((waits, fn, (eng, 1)))
        self._commit(tok, reads, writes)
        self.n_ops += 1
        return tok

    def dma(self, fn, reads=(), writes=(), queue=SP):
        deps = self._deps(reads, writes)
        lane = self.next_lane
        self.next_lane = (self.next_lane + 1) % N_LANES
        lk = ("lane", lane)
        if self.lane_cnt[lane] > 0 and deps.get(lk, 0) < self.lane_cnt[lane]:
            deps[lk] = self.lane_cnt[lane]
        waits = self._filter(queue, deps)
        self.lane_cnt[lane] += 16
        tok = (lk, self.lane_cnt[lane])
        self.ops[queue].append((waits, fn, (lk, 16)))
        self._commit(tok, reads, writes)
        self.n_ops += 1
        return tok

    def barrier(self):
        targets = {}
        for e in COMPUTE:
            if self.cnt[e] > 0:
                targets[e] = self.cnt[e]
        for i, c in enumerate(self.lane_cnt):
            if c > 0:
                targets[("lane", i)] = c
        for e in (PE, ACT, DVE, POOL, SP):
            waits = []
            for k, v in targets.items():
                if k == e or self.waited[e].get(k, 0) >= v:
                    continue
                self.waited[e][k] = v
                waits.append((k, v))
            if waits:
                self.ops[e].append((waits, None, None))

    def emit(self, nc, stack):
        sems = {}
        for e in COMPUTE:
            sems[e] = stack.enter_context(nc.semaphore("s_" + e))
        for i in range(N_LANES):
            sems[("lane", i)] = stack.enter_context(nc.semaphore("s_lane%d" % i))
        block = stack.enter_context(nc.Block())
        ops = self.ops

        def run(name):
            def body(eng):
                for waits, fn, inc in ops[name]:
                    for k, v in waits:
                        eng.wait_ge(sems[k], v)
                    if fn is not None:
                        fn(eng).then_inc(sems[inc[0]], inc[1])
            return body

        block.sync(run(SP))
        block.tensor(run(PE))
        block.scalar(run(ACT))
        block.vector(run(DVE))
        block.gpsimd(run(POOL))


class Arena:
    def __init__(self, nc, base=16384, limit=229376 - 256):
        self.nc, self.base, self.off, self.limit, self.n = nc, base, base, limit, 0

    def alloc(self, name, shape, dtype):
        esz = 4 if dtype == F32 else 2
        n = 1
        for s in shape[1:]:
            n *= s
        nbytes = (n * esz + 63) // 64 * 64
        off = self.off
        assert off + nbytes <= self.limit, ("SBUF overflow", name, off, nbytes, self.limit)
        self.off += nbytes
        self.n += 1
        t = self.nc.alloc_sbuf_tensor_at("%s_%d" % (name, self.n), list(shape), dtype, offset=off)
        return t, Buf(name)


class K:
    def __init__(self, S):
        self.S = S

    def mm(self, out, lhsT, rhs, start, stop, reads, writes, skip=False):
        if skip:
            self.S.op(PE, lambda e: e.matmul(out, lhsT=lhsT, rhs=rhs, start=start, stop=stop, skip_group_check=True),
                      reads, writes)
        else:
            self.S.op(PE, lambda e: e.matmul(out, lhsT=lhsT, rhs=rhs, start=start, stop=stop), reads, writes)

    def act(self, out, in_, func, reads, writes, scale=1.0, bias=None):
        if bias is None:
            self.S.op(ACT, lambda e: e.activation(out=out, in_=in_, func=func, scale=scale), reads, writes)
        else:
            self.S.op(ACT, lambda e: e.activation(out=out, in_=in_, func=func, scale=scale, bias=bias), reads, writes)

    def tt(self, eng, out, in0, in1, op, reads, writes):
        self.S.op(eng, lambda e: e.tensor_tensor(out=out, in0=in0, in1=in1, op=op), reads, writes)

    def ts(self, eng, out, in0, s1, s2, op0, op1, reads, writes, accum_out=None):
        if op1 is None:
            self.S.op(eng, lambda e: e.tensor_scalar(out=out, in0=in0, scalar1=s1, scalar2=None, op0=op0), reads, writes)
        elif accum_out is None:
            self.S.op(eng, lambda e: e.tensor_scalar(out=out, in0=in0, scalar1=s1, scalar2=s2, op0=op0, op1=op1), reads, writes)
        else:
            self.S.op(eng, lambda e: e.tensor_scalar(out=out, in0=in0, scalar1=s1, scalar2=s2, op0=op0, op1=op1,
                                                     accum_out=accum_out), reads, writes)

    def stt(self, eng, out, in0, scalar, in1, op0, op1, reads, writes):
        self.S.op(eng, lambda e: e.scalar_tensor_tensor(out=out, in0=in0, scalar=scalar, in1=in1, op0=op0, op1=op1),
                  reads, writes)

    def cp(self, eng, out, in_, reads, writes):
        self.S.op(eng, lambda e: e.tensor_copy(out=out, in_=in_), reads, writes)

    def memset(self, eng, ap, val, writes):
        self.S.op(eng, lambda e: e.memset(ap, val), (), writes)

    def recip(self, out, in_, reads, writes):
        self.S.op(DVE, lambda e: e.reciprocal(out=out, in_=in_), reads, writes)

    def reduce(self, out, in_, op, reads, writes):
        self.S.op(DVE, lambda e: e.tensor_reduce(out=out, in_=in_, axis=AX.X, op=op), reads, writes)

    def dma(self, out, in_, reads, writes, queue=SP):
        self.S.dma(lambda e: e.dma_start(out=out, in_=in_), reads, writes, queue=queue)


def host_constants(S):
    f32 = np.float32
    inv = (1.0 / (np.float32(10000.0) ** (np.arange(0, 64, 2, dtype=f32) / f32(64)))).astype(f32)
    ang = (np.arange(S, dtype=f32)[:, None] * inv[None, :]).astype(f32)
    cos = np.cos(ang).astype(f32).T
    sin = np.sin(ang).astype(f32).T
    cosT = np.ascontiguousarray(np.tile(cos, (4, 1)))
    sinT = np.ascontiguousarray(np.tile(sin, (4, 1)))
    ident = np.eye(128, dtype=f32)
    rot = np.zeros((128, 128), f32)
    for m in range(128):
        if (m % 64) < 32:
            rot[m + 32, m] = -1.0
        else:
            rot[m - 32, m] = 1.0
    bones = np.zeros((128, 128), f32)
    bones[0:64, 0:64] = 1.0
    bones[64:128, 64:128] = 1.0
    ones = np.ones((128, 128), f32)
    negm = np.where(np.arange(128)[None, :] <= np.arange(128)[:, None], 0.0, NEG).astype(f32)
    corr = np.ones((128, 4, 16), f32)
    for gi, w in enumerate(POOL_WINDOWS):
        t = np.arange(16)
        corr[:, gi, :] = (w / np.minimum(t + 1, w)).astype(f32)[None, :]
    cmat = np.concatenate([ident, rot, bones, ones, negm, corr.reshape(128, 64)], axis=1)
    return cosT, sinT, np.ascontiguousarray(cmat)


def host_vectors(norm_g, conv_w, q_norm_g, k_norm_g, pool_scale):
    L = norm_g.shape[0]
    ng = norm_g.reshape(L, 8, 128).transpose(2, 0, 1).reshape(128, L * 8)
    cw = conv_w.reshape(L, 3, 4, 128).transpose(3, 0, 1, 2).reshape(128, L * 12)
    qg = np.tile(q_norm_g.T, (2, 1))
    kg = np.tile(k_norm_g.T, (2, 1))
    psc = pool_scale.reshape(L, 4, 128).transpose(2, 0, 1).reshape(128, L * 4)
    return np.ascontiguousarray(np.concatenate([ng, cw, qg, kg, psc], axis=1).astype(np.float32))


def build_program(S, L, debug=False, phases="PIA"):
    NQ = S // 128
    TS = 2048 if S % 2048 == 0 else S
    NST = S // TS
    NSUB = TS // 512
    topk = min(TOPK, S // 4)
    nc = bass.Bass("TRN2", target_bir_lowering=False)

    def din(name, shape, dt=F32):
        return nc.dram_tensor(name, list(shape), dt, kind="ExternalInput").ap()

    dbg_kind = "ExternalOutput" if debug else "Internal"

    def dscr(name, shape, dt):
        return nc.dram_tensor(name, list(shape), dt, kind=dbg_kind).ap()

    xT_in = din("xT", [D, S])
    w_in = din("w_in", [L, D, INW])
    w_oc = din("w_out_conv", [L, 512, D])
    w_oa = din("w_out_attn", [L, 512, D])
    w_op = din("w_out_pool", [L, 512, D])
    w_o = din("w_o", [L, D, D])
    pool_w = din("pool_w", [L, 4, 128, 128])
    vecs = din("vecs", [128, L * 26])
    cosT = din("cosT", [128, S])
    sinT = din("sinT", [128, S])
    cmat = din("cmat", [128, 704])
    outT = nc.dram_tensor("outT", [D, S], F32, kind="ExternalOutput").ap()

    xbuf = nc.dram_tensor("xbuf", [D, S], F32, kind="Internal").ap()
    QT = dscr("QT", [4, 128, S], BF16)
    KT = dscr("KT", [4, 128, S], BF16)
    IQT = dscr("IQT", [4, 128, S], BF16)
    IKT = dscr("IKT", [64, S], BF16)
    VA = dscr("VA", [S, 520], BF16)
    IW = dscr("IW", [S, 8], F32)
    SG = dscr("SG", [4, 128, S], BF16)
    MG1 = dscr("MG1", [8, 128, S], BF16)
    MACC = dscr("MACC", [8, 128, S], F32)
    MASKT = dscr("MASKT", [NQ, 128, S], BF16)

    S_ = Sched()
    k = K(S_)
    ar = Arena(nc)
    stack = ExitStack()
    ps = [stack.enter_context(nc.psum_tensor("ps%d" % i, [128, 512], F32)) for i in range(8)]
    psb = [Buf("ps%d" % i) for i in range(8)]
    dram = Buf("dram")

    cst_f, cst_fb = ar.alloc("cst_f", [128, 704], F32)
    cst_b, cst_bb = ar.alloc("cst_b", [128, 512], BF16)
    vec, vecb = ar.alloc("vec", [128, L * 26], F32)
    eps_t, eps_b = ar.alloc("eps", [128, 2], F32)
    k.dma(cst_f[:], cmat, [], [cst_fb])
    k.dma(vec[:], vecs, [], [vecb])
    k.cp(DVE, cst_b[:], cst_f[:, 0:512], [cst_fb], [cst_bb])
    k.memset(POOL, eps_t[:, 0:1], RMS_EPS, [eps_b])
    k.memset(POOL, eps_t[:, 1:2], -1.0e29, [eps_b])
    identb = cst_b[:, 0:128]
    rotb = cst_b[:, 128:256]
    bonesb = cst_b[:, 256:384]
    onesb = cst_b[:, 384:512]
    negm = cst_f[:, 512:640]
    corr = cst_f[:, 640:704]
    CB = [cst_bb, cst_fb, vecb, eps_b]

    def v_ng(l, kc):
        return vec[:, l * 8 + kc: l * 8 + kc + 1]

    def v_cw(l, j, i):
        o = L * 8 + l * 12 + j * 4 + i
        return vec[:, o:o + 1]

    def v_qg(l):
        o = L * 20 + l
        return vec[:, o:o + 1]

    def v_kg(l):
        o = L * 21 + l
        return vec[:, o:o + 1]

    def v_psc(l, i):
        o = L * 22 + l * 4 + i
        return vec[:, o:o + 1]

    base_off = ar.off
    bank_rr = [0]

    def nbank():
        b = bank_rr[0]
        bank_rr[0] = (b + 1) % 8
        return b

    def phase_P(l, x_src):
        ar.off = base_off
        hT, hTb = ar.alloc("hT", [128, 8, TS], BF16)
        yA, yAb = ar.alloc("yA", [128, 4, TS], BF16)
        yC, yCb = ar.alloc("yC", [128, 4, TS], BF16)
        xt, xtb = ar.alloc("xt", [128, 8, 512], F32)
        sq, sqb = ar.alloc("sq", [128, 8, 512], BF16)
        rstd, rstdb = ar.alloc("rstd", [128, 512], F32)
        cs, csb = ar.alloc("cs", [128, TS], F32)
        sn, snb = ar.alloc("sn", [128, TS], F32)
        NWS = 4
        wst = [ar.alloc("wst%d" % i, [128, 8, 128], F32) for i in range(NWS)]
        NWB = 8
        wbf_all, _ = ar.alloc("wbf", [128, NWB, 8, 128], BF16)
        wbfb = [Buf("wbf%d" % i) for i in range(NWB)]
        zbuf, zb = ar.alloc("zbuf", [128, 4, 514], F32)
        ubuf, ub = ar.alloc("ubuf", [128, 4, 528], F32)
        NT = 6
        tf = [ar.alloc("tf%d" % i, [128, 528], F32) for i in range(NT)]
        NTB = 6
        tb = [ar.alloc("tb%d" % i, [128, 512], BF16) for i in range(NTB)]
        vst = [ar.alloc("vst%d" % i, [128, 8, 65], BF16) for i in range(2)]
        iwst, iwstb = ar.alloc("iwst", [128, TS // 128, 8], F32)
        rr = {"ws": 0, "wb": 0, "tf": 0, "tb": 0}

        def ntf():
            i = rr["tf"]
            rr["tf"] = (i + 1) % NT
            return tf[i]

        def ntb():
            i = rr["tb"]
            rr["tb"] = (i + 1) % NTB
            return tb[i]

        def load_w(src, kcn, ncols):
            si = rr["ws"]
            rr["ws"] = (si + 1) % NWS
            bi = rr["wb"]
            rr["wb"] = (bi + 1) % NWB
            st_t, st_b = wst[si]
            k.dma(st_t[:, 0:kcn, 0:ncols], src.rearrange("(kc p) c -> p kc c", p=128), [], [st_b])
            k.cp(POOL, wbf_all[:, bi, 0:kcn, 0:ncols], st_t[:, 0:kcn, 0:ncols], [st_b], [wbfb[bi]])
            return bi

        def proj(bi, sub, ncols=128, bank=None):
            b = nbank() if bank is None else bank
            for kc in range(8):
                k.mm(ps[b][0:ncols, :], wbf_all[:, bi, kc, 0:ncols], hT[:, kc, sub * 512:(sub + 1) * 512],
                     kc == 0, kc == 7, [wbfb[bi], hTb], [psb[b]])
            return b

        k.memset(POOL, zbuf[:], 0.0, [zb])
        k.memset(POOL, ubuf[:], 0.0, [ub])
        for i in range(2):
            k.memset(POOL, vst[i][0][:], 1.0, [vst[i][1]])

        for st in range(NST):
            T0 = st * TS
            for sub in range(NSUB):
                t0 = T0 + sub * 512
                k.dma(xt[:], x_src[:, t0:t0 + 512].rearrange("(kc p) t -> p kc t", p=128), [dram], [xtb])
                for kc in range(8):
                    k.act(sq[:, kc, :], xt[:, kc, :], AF.Square, [xtb], [sqb])
                b = nbank()
                for kc in range(8):
                    k.mm(ps[b][:, :], onesb, sq[:, kc, :], kc == 0, kc == 7, [sqb] + CB, [psb[b]])
                k.act(rstd[:], ps[b][:, :], AF.Sqrt, [psb[b]] + CB, [rstdb], scale=1.0 / D, bias=eps_t[:, 0:1])
                k.recip(rstd[:], rstd[:], [rstdb], [rstdb])
                for kc in range(8):
                    k.stt(DVE, hT[:, kc, sub * 512:(sub + 1) * 512], xt[:, kc, :], v_ng(l, kc), rstd[:],
                          ALU.mult, ALU.mult, [xtb, rstdb] + CB, [hTb])
            k.dma(cs[:], cosT[:, T0:T0 + TS], [], [csb])
            k.dma(sn[:], sinT[:, T0:T0 + TS], [], [snb])

            for i in range(4):
                wb_ = [load_w(w_in[l, :, c0 + i * 128: c0 + (i + 1) * 128], 8, 128) for c0 in (C_CB, C_CC, C_CX, C_CG)]
                for sub in range(NSUB):
                    b_cb, b_cc, b_cx, b_cg = [proj(w, sub) for w in wb_]
                    t_cc, t_ccb = ntf()
                    k.act(t_cc[:, 0:512], ps[b_cc][:, :], AF.Copy, [psb[b_cc]], [t_ccb])
                    k.tt(DVE, zbuf[:, i, 2:514], ps[b_cx][:, :], t_cc[:, 0:512], ALU.mult, [psb[b_cx], t_ccb], [zb])
                    acc, accb = ntf()
                    k.ts(DVE, acc[:, 0:512], zbuf[:, i, 0:512], v_cw(l, 0, i), None, ALU.mult, None, [zb] + CB, [accb])
                    k.stt(DVE, acc[:, 0:512], zbuf[:, i, 1:513], v_cw(l, 1, i), acc[:, 0:512], ALU.mult, ALU.add,
                          [zb, accb] + CB, [accb])
                    k.stt(DVE, acc[:, 0:512], zbuf[:, i, 2:514], v_cw(l, 2, i), acc[:, 0:512], ALU.mult, ALU.add,
                          [zb, accb] + CB, [accb])
                    k.cp(POOL, zbuf[:, i, 0:2], zbuf[:, i, 512:514], [zb], [zb])
                    sg, sgb = ntf()
                    k.act(sg[:, 0:512], ps[b_cg][:, :], AF.Sigmoid, [psb[b_cg]], [sgb])
                    k.tt(DVE, sg[:, 0:512], ps[b_cg][:, :], sg[:, 0:512], ALU.mult, [psb[b_cg], sgb], [sgb])
                    k.tt(DVE, sg[:, 0:512], ps[b_cb][:, :], sg[:, 0:512], ALU.mult, [psb[b_cb], sgb], [sgb])
                    k.tt(POOL, yA[:, i, sub * 512:(sub + 1) * 512], sg[:, 0:512], acc[:, 0:512], ALU.mult,
                         [sgb, accb], [yAb])

            for i in range(4):
                w = POOL_WINDOWS[i]
                w_pu = load_w(w_in[l, :, C_PU + i * 128: C_PU + (i + 1) * 128], 8, 128)
                w_pg = load_w(w_in[l, :, C_PG + i * 128: C_PG + (i + 1) * 128], 8, 128)
                w_pw = load_w(pool_w[l, i], 1, 128)
                for sub in range(NSUB):
                    b_pu = proj(w_pu, sub)
                    b_pg = proj(w_pg, sub)
                    k.act(ubuf[:, i, 16:528], ps[b_pu][:, :], AF.Copy, [psb[b_pu]], [ub])
                    cur, curb = None, None
                    src, srcb = ubuf[:, i, :], ub
                    sh = 1
                    while sh < w:
                        nt_, ntb_ = ntf()
                        k.tt(POOL, nt_[:, sh:528], src[:, sh:528], src[:, 0:528 - sh], ALU.add, [srcb], [ntb_])
                        if sh == 1:
                            pass
                        src, srcb = nt_, ntb_
                        sh *= 2
                    first = (st == 0 and sub == 0)
                    pooled, pooledb = ntb()
                    if first:
                        m_, m_b = ntf()
                        k.ts(DVE, m_[:, 0:512], src[:, 16:528], 1.0 / w, None, ALU.mult, None, [srcb], [m_b])
                        k.tt(DVE, m_[:, 0:16], m_[:, 0:16], corr[:, i * 16:(i + 1) * 16], ALU.mult, [m_b] + CB, [m_b])
                        k.tt(DVE, pooled[:], m_[:, 0:512], ubuf[:, i, 16:528], ALU.subtract, [m_b, ub], [pooledb])
                    else:
                        k.stt(DVE, pooled[:], src[:, 16:528], 1.0 / w, ubuf[:, i, 16:528], ALU.mult, ALU.subtract,
                              [srcb, ub], [pooledb])
                    k.cp(POOL, ubuf[:, i, 0:16], ubuf[:, i, 512:528], [ub, srcb], [ub])
                    b_m = nbank()
                    k.mm(ps[b_m][:, :], wbf_all[:, w_pw, 0, :], pooled[:], True, True, [wbfb[w_pw], pooledb], [psb[b_m]])
                    sg, sgb = ntf()
                    k.act(sg[:, 0:512], ps[b_pg][:, :], AF.Sigmoid, [psb[b_pg]], [sgb])
                    k.tt(DVE, sg[:, 0:512], ps[b_pg][:, :], sg[:, 0:512], ALU.mult, [psb[b_pg], sgb], [sgb])
                    k.stt(DVE, yC[:, i, sub * 512:(sub + 1) * 512], ps[b_m][:, :], v_psc(l, i), sg[:, 0:512],
                          ALU.mult, ALU.mult, [psb[b_m], sgb] + CB, [yCb])

            for j in range(8):
                w_a = load_w(w_oc[l, :, j * 128:(j + 1) * 128], 4, 128)
                w_c = load_w(w_op[l, :, j * 128:(j + 1) * 128], 4, 128)
                w_g0 = load_w(w_in[l, :, C_MG + j * 128: C_MG + (j + 1) * 128], 8, 128)
                w_g2 = load_w(w_in[l, :, C_MG + 2048 + j * 128: C_MG + 2048 + (j + 1) * 128], 8, 128)
                for sub in range(NSUB):
                    sl = slice(sub * 512, (sub + 1) * 512)
                    bA = nbank()
                    for kc in range(4):
                        k.mm(ps[bA][:, :], wbf_all[:, w_a, kc, :], yA[:, kc, sl], kc == 0, kc == 3, [wbfb[w_a], yAb], [psb[bA]])
                    bG0 = proj(w_g0, sub)
                    bC = nbank()
                    for kc in range(4):
                        k.mm(ps[bC][:, :], wbf_all[:, w_c, kc, :], yC[:, kc, sl], kc == 0, kc == 3, [wbfb[w_c], yCb], [psb[bC]])
                    bG2 = proj(w_g2, sub)
                    sA, sAb = ntf()
                    sC, sCb = ntf()
                    k.act(sA[:, 0:512], ps[bG0][:, :], AF.Sigmoid, [psb[bG0]], [sAb])
                    k.act(sC[:, 0:512], ps[bG2][:, :], AF.Sigmoid, [psb[bG2]], [sCb])
                    k.tt(DVE, sA[:, 0:512], ps[bA][:, :], sA[:, 0:512], ALU.mult, [psb[bA], sAb], [sAb])
                    k.tt(DVE, sC[:, 0:512], ps[bC][:, :], sC[:, 0:512], ALU.mult, [psb[bC], sCb], [sCb])
                    k.tt(POOL, sA[:, 0:512], sA[:, 0:512], sC[:, 0:512], ALU.add, [sAb, sCb], [sAb])
                    t0 = T0 + sub * 512
                    k.dma(MACC[j, :, t0:t0 + 512], sA[:, 0:512], [sAb], [dram])

            for (c0, dst, gfun) in ((C_Q, QT, v_qg), (C_K, KT, v_kg)):
                for c in range(4):
                    wq = load_w(w_in[l, :, c0 + c * 128: c0 + (c + 1) * 128], 8, 128)
                    for sub in range(NSUB):
                        sl = slice(sub * 512, (sub + 1) * 512)
                        b = proj(wq, sub)
                        qg_, qgb = ntb()
                        sq_, sq_b = ntb()
                        k.act(qg_[:], ps[b][:, :], AF.Copy, [psb[b]] + CB, [qgb], scale=gfun(l))
                        k.act(sq_[:], ps[b][:, :], AF.Square, [psb[b]], [sq_b])
                        bS = nbank()
                        k.mm(ps[bS][:, :], bonesb, sq_[:], True, True, [sq_b] + CB, [psb[bS]])
                        bR = nbank()
                        k.mm(ps[bR][:, :], rotb, qg_[:], True, True, [qgb] + CB, [psb[bR]])
                        t1, t1b = ntf()
                        t2, t2b = ntf()
                        k.tt(DVE, t1[:, 0:512], qg_[:], cs[:, sl], ALU.mult, [qgb, csb], [t1b])
                        k.tt(DVE, t2[:, 0:512], ps[bR][:, :], sn[:, sl], ALU.mult, [psb[bR], snb], [t2b])
                        k.tt(POOL, t1[:, 0:512], t1[:, 0:512], t2[:, 0:512], ALU.add, [t1b, t2b], [t1b])
                        k.act(t2[:, 0:512], ps[bS][:, :], AF.Sqrt, [psb[bS], t2b] + CB, [t2b], scale=1.0 / 64,
                              bias=eps_t[:, 0:1])
                        k.recip(t2[:, 0:512], t2[:, 0:512], [t2b], [t2b])
                        o_, ob = ntb()
                        k.tt(DVE, o_[:], t1[:, 0:512], t2[:, 0:512], ALU.mult, [t1b, t2b], [ob])
                        t0 = T0 + sub * 512
                        k.dma(dst[c, :, t0:t0 + 512], o_[:], [ob], [dram])

            for c in range(5):
                ncols = 128 if c < 4 else 64
                col = C_IQ + c * 128 if c < 4 else C_IK
                wq = load_w(w_in[l, :, col: col + ncols], 8, ncols)
                for sub in range(NSUB):
                    sl = slice(sub * 512, (sub + 1) * 512)
                    b = proj(wq, sub, ncols)
                    b_, bb = ntb()
                    k.act(b_[0:ncols, :], ps[b][0:ncols, :], AF.Copy, [psb[b]], [bb])
                    bR = nbank()
                    k.mm(ps[bR][0:ncols, :], cst_b[0:ncols, 128:128 + ncols], b_[0:ncols, :], True, True, [bb] + CB, [psb[bR]])
                    t1, t1b = ntf()
                    t2, t2b = ntf()
                    k.tt(DVE, t1[0:ncols, 0:512], b_[0:ncols, :], cs[0:ncols, sl], ALU.mult, [bb, csb], [t1b])
                    k.tt(DVE, t2[0:ncols, 0:512], ps[bR][0:ncols, :], sn[0:ncols, sl], ALU.mult, [psb[bR], snb], [t2b])
                    o_, ob = ntb()
                    k.tt(POOL, o_[0:ncols, :], t1[0:ncols, 0:512], t2[0:ncols, 0:512], ALU.add, [t1b, t2b], [ob])
                    t0 = T0 + sub * 512
                    if c < 4:
                        k.dma(IQT[c, :, t0:t0 + 512], o_[:], [ob], [dram])
                    else:
                        k.dma(IKT[:, t0:t0 + 512], o_[0:64, :], [ob], [dram])

            for c in range(4):
                wq = load_w(w_in[l, :, C_AG + c * 128: C_AG + (c + 1) * 128], 8, 128)
                for sub in range(NSUB):
                    b = proj(wq, sub)
                    sg, sgb = ntf()
                    k.act(sg[:, 0:512], ps[b][:, :], AF.Sigmoid, [psb[b]], [sgb])
                    o_, ob = ntb()
                    k.tt(DVE, o_[:], ps[b][:, :], sg[:, 0:512], ALU.mult, [psb[b], sgb], [ob])
                    t0 = T0 + sub * 512
                    k.dma(SG[c, :, t0:t0 + 512], o_[:], [ob], [dram])
            for j in range(8):
                wq = load_w(w_in[l, :, C_MG + 1024 + j * 128: C_MG + 1024 + (j + 1) * 128], 8, 128)
                for sub in range(NSUB):
                    b = proj(wq, sub)
                    o_, ob = ntb()
                    k.act(o_[:], ps[b][:, :], AF.Sigmoid, [psb[b]], [ob])
                    t0 = T0 + sub * 512
                    k.dma(MG1[j, :, t0:t0 + 512], o_[:], [ob], [dram])

            wv = [load_w(w_in[l, :, C_V + c * 128: C_V + (c + 1) * 128], 8, 128) for c in range(4)]
            wiw = load_w(w_in[l, :, C_IW: C_IW + 8], 8, 8)
            for tt_ in range(TS // 128):
                tsl = slice(tt_ * 128, (tt_ + 1) * 128)
                vt, vtb = vst[tt_ % 2]
                for c in range(4):
                    b = nbank()
                    for kc in range(8):
                        k.mm(ps[b][:, 0:128], hT[:, kc, tsl], wbf_all[:, wv[c], kc, :], kc == 0, kc == 7,
                             [hTb, wbfb[wv[c]]], [psb[b]])
                    k.act(vt[:, 2 * c:2 * c + 2, 0:64], ps[b][:, 0:128].rearrange("p (h d) -> p h d", h=2), AF.Copy,
                          [psb[b]], [vtb])
                tok = T0 + tt_ * 128
                k.dma(VA[tok:tok + 128, :], vt[:].rearrange("p h d -> p (h d)"), [vtb], [dram])
                b = nbank()
                for kc in range(8):
                    k.mm(ps[b][:, 0:8], hT[:, kc, tsl], wbf_all[:, wiw, kc, 0:8], kc == 0, kc == 7,
                         [hTb, wbfb[wiw]], [psb[b]])
                k.cp(DVE, iwst[:, tt_, :], ps[b][:, 0:8], [psb[b]], [iwstb])
            k.dma(IW[T0:T0 + TS, :].rearrange("(t p) h -> p t h", p=128), iwst[:], [iwstb], [dram])
        S_.barrier()

    def phase_I(l):
        ar.off = base_off
        G = 3
        ALPHA = 0.0
        ik2, ik2b = ar.alloc("ik2", [128, S], BF16)
        scores = [ar.alloc("score%d" % i, [128, S], F32) for i in range(G)]
        masks = [ar.alloc("mask%d" % i, [128, S], BF16) for i in range(G)]
        iqt = [ar.alloc("iqt%d" % i, [128, 4, 128], BF16) for i in range(2)]
        iwt = [ar.alloc("iwt%d" % i, [128, 8], F32) for i in range(2)]
        diag = [ar.alloc("diag%d" % i, [128, 8, 128], BF16) for i in range(2)]
        NR = 6
        rt = [ar.alloc("rt%d" % i, [128, 512], BF16) for i in range(NR)]
        mts = [ar.alloc("mts%d" % i, [128, 512], BF16) for i in range(3)]
        st, _ = ar.alloc("bst", [128, 8, G], F32)
        b_lw, b_mid, b_cnt, b_ca, b_thr = Buf("lw"), Buf("mid"), Buf("cnt"), Buf("ca"), Buf("thr")
        b_mA = [Buf("mA%d" % i) for i in range(G)]
        b_mD = [Buf("mD%d" % i) for i in range(G)]
        LO, WD, MID, CNT, CA, TOT, SW, THR = range(8)
        rri = {"r": 0, "m": 0, "e": 0, "o": 0, "s": 0, "t": 0}
        k.dma(ik2[0:64, :], IKT, [dram], [ik2b])
        k.dma(ik2[64:128, :], IKT, [dram], [ik2b])
        k.memset(POOL, st[:], 0.0, [b_lw, b_mid, b_cnt, b_ca, b_thr])

        def indexer(qt, score, scoreb):
            t0 = qt * 128
            Lk = t0 + 128
            iq_t, iq_b = iqt[qt % 2]
            iw_t, iw_b = iwt[qt % 2]
            dg, dgb = diag[qt % 2]
            k.dma(iq_t[:], IQT[:, :, t0:t0 + 128].rearrange("c p t -> p c t"), [dram], [iq_b])
            k.dma(iw_t[:], IW[t0:t0 + 128, :], [dram], [iw_b])
            k.ts(POOL, iw_t[:], iw_t[:], 0.125 * (8.0 ** -0.5), None, ALU.mult, None, [iw_b], [iw_b])
            for h in range(8):
                k.act(dg[:, h, :], identb, AF.Copy, [iw_b] + CB, [dgb], scale=iw_t[:, h:h + 1])
            for kc in range((Lk + 511) // 512):
                n = min(512, Lk - kc * 512)
                ksl = slice(kc * 512, kc * 512 + n)
                bsc = 4 + rri["s"]
                rri["s"] ^= 1
                for h in range(8):
                    par = h % 2
                    if par == 0:
                        bx = rri["e"]
                        rri["e"] ^= 1
                    else:
                        bx = 2 + rri["o"]
                        rri["o"] ^= 1
                    pb = par * 64
                    k.mm(ps[bx][:, 0:n], iq_t[pb:pb + 64, h // 2, :], ik2[pb:pb + 64, ksl], True, True,
                         [iq_b, ik2b], [psb[bx]])
                    r_, r_b = rt[rri["r"]]
                    rri["r"] = (rri["r"] + 1) % NR
                    k.act(r_[:, 0:n], ps[bx][:, 0:n], AF.Relu, [psb[bx]], [r_b])
                    k.mm(ps[bsc][:, 0:n], dg[:, h, :], r_[:, 0:n], h == 0, h == 7, [dgb, r_b], [psb[bsc]])
                k.act(score[:, ksl], ps[bsc][:, 0:n], AF.Copy, [psb[bsc]], [scoreb])
            k.tt(POOL, score[:, Lk - 128:Lk], score[:, Lk - 128:Lk], negm, ALU.add, [scoreb] + CB, [scoreb])

        def select(qts):
            ng = len(qts)
            active = [gi for gi, qt in enumerate(qts) if qt * 128 >= topk]
            Las = {}
            for gi in active:
                Lk = qts[gi] * 128 + 128
                score, scoreb = scores[gi]
                Las[gi] = int(Lk * ALPHA) // 64 * 64
                k.reduce(st[:, WD, gi:gi + 1], score[:, 0:Lk], ALU.max, [scoreb], [b_lw])
                k.reduce(st[:, LO, gi:gi + 1], score[:, 0:Lk - 128], ALU.min, [scoreb], [b_lw])
                k.memset(POOL, st[:, THR, gi:gi + 1], float(topk) - 0.5 - Las[gi] / 2.0, [b_thr])
            if active:
                k.tt(DVE, st[:, WD, :], st[:, WD, :], st[:, LO, :], ALU.subtract, [b_lw], [b_lw])
                for it in range(NITER):
                    k.ts(DVE, st[:, WD, :], st[:, WD, :], 0.5, None, ALU.mult, None, [b_lw], [b_lw])
                    k.tt(DVE, st[:, MID, :], st[:, LO, :], st[:, WD, :], ALU.add, [b_lw], [b_mid])
                    for gi in active:
                        Lk = qts[gi] * 128 + 128
                        La = Las[gi]
                        score, scoreb = scores[gi]
                        mask, maskb = masks[gi]
                        mid = st[:, MID, gi:gi + 1]
                        if La > 0:
                          S_.op(ACT, (lambda o_, i_, b_, a_: (lambda e: e.activation(out=o_, in_=i_, func=AF.Sign, scale=-1.0,
                                                                                  bias=b_, accum_out=a_)))(
                              mask[:, 0:La], score[:, 0:La], mid, st[:, CA, gi:gi + 1]), [scoreb, b_mid], [b_mA[gi], b_ca])
                        k.ts(DVE, mask[:, La:Lk], score[:, La:Lk], mid, None, ALU.is_ge, ALU.add, [scoreb, b_mid],
                             [b_mD[gi], b_cnt], accum_out=st[:, CNT, gi:gi + 1])
                    k.stt(DVE, st[:, TOT, :], st[:, CA, :], -0.5, st[:, CNT, :], ALU.mult, ALU.add, [b_ca, b_cnt], [b_cnt])
                    k.tt(DVE, st[:, SW, :], st[:, TOT, :], st[:, THR, :], ALU.is_ge, [b_cnt, b_thr], [b_cnt])
                    k.tt(DVE, st[:, SW, :], st[:, SW, :], st[:, WD, :], ALU.mult, [b_cnt, b_lw], [b_cnt])
                    k.tt(DVE, st[:, LO, :], st[:, LO, :], st[:, SW, :], ALU.add, [b_lw, b_cnt], [b_lw])
            for gi, qt in enumerate(qts):
                Lk = qt * 128 + 128
                score, scoreb = scores[gi]
                mask, maskb = masks[gi]
                if gi in active:
                    thr, thr_r = st[:, LO, gi:gi + 1], [b_lw]
                else:
                    thr, thr_r = eps_t[:, 1:2], CB
                k.ts(DVE, mask[:, 0:Lk], score[:, 0:Lk], thr, None, ALU.is_ge, None, [scoreb] + thr_r,
                     [maskb, b_mA[gi], b_mD[gi]])
                for g0 in range(0, qt + 1, 4):
                    ng_ = min(4, qt + 1 - g0)
                    bt = 6 + rri["t"]
                    rri["t"] ^= 1
                    for j in range(ng_):
                        kt = g0 + j
                        k.mm(ps[bt][:, j * 128:(j + 1) * 128], mask[:, kt * 128:(kt + 1) * 128], identb, True, True,
                             [maskb] + CB, [psb[bt]])
                    m_, m_b = mts[rri["m"]]
                    rri["m"] = (rri["m"] + 1) % 3
                    k.cp(DVE, m_[:, 0:ng_ * 128], ps[bt][:, 0:ng_ * 128], [psb[bt]], [m_b])
                    k.dma(MASKT[qt, :, g0 * 128:(g0 + ng_) * 128], m_[:, 0:ng_ * 128], [m_b], [dram])

        groups = [list(range(g, min(g + G, NQ))) for g in range(0, NQ, G)]
        for qts in groups:
            for gi, qt in enumerate(qts):
                indexer(qt, scores[gi][0], scores[gi][1])
            select(qts)
        S_.barrier()

    def phase_A(l, x_src, x_dst):
        ar.off = base_off
        kt2, kt2b = ar.alloc("kt2", [128, 4, S], BF16)
        vsb, vsbb = ar.alloc("vsb", [128, NQ, 520], BF16)
        woa, woab = ar.alloc("woa", [128, 4, D], BF16)
        wo, wob = ar.alloc("wo", [128, 8, D], BF16)
        mkt, mktb = ar.alloc("mkt", [128, S], BF16)
        wst = [ar.alloc("awst%d" % i, [128, 1024], F32) for i in range(2)]
        qtt = [ar.alloc("qtt%d" % i, [128, 4, 128], BF16) for i in range(2)]
        NPT = 4
        ptl = [ar.alloc("pt%d" % i, [128, 4, 128], BF16) for i in range(NPT)]
        rden, rdenb = ar.alloc("rden", [128, 8], F32)
        on, onb = ar.alloc("on", [128, 512], BF16)
        ogT, ogTb = ar.alloc("ogT", [128, 4, 128], BF16)
        sgt, sgtb = ar.alloc("sgt", [128, 4, 128], BF16)
        mg1t, mg1tb = ar.alloc("mg1t", [128, 8, 128], BF16)
        macct, macctb = ar.alloc("macct", [128, 8, 128], F32)
        xtl, xtlb = ar.alloc("xtl", [128, 8, 128], F32)
        mT, mTb = ar.alloc("mT", [128, 8, 128], BF16)
        mtmp, mtmpb = wst[0][0][:, 0:1024].rearrange("p (c q) -> p c q", c=8), wst[0][1]
        for c in range(4):
            k.dma(kt2[:, c, :], KT[c], [dram], [kt2b])
        vav = VA.rearrange("(t p) f -> p t f", p=128)
        for g0 in range(0, NQ, 8):
            g1 = min(NQ, g0 + 8)
            k.dma(vsb[:, g0:g1, :], vav[:, g0:g1, :], [dram], [vsbb])
        wi = 0
        for kc in range(4):
            w_t, w_b = wst[wi % 2]
            wi += 1
            k.dma(w_t[:], w_oa[l, kc * 128:(kc + 1) * 128, :], [], [w_b])
            k.cp(POOL, woa[:, kc, :], w_t[:], [w_b], [woab])
        for kc in range(8):
            w_t, w_b = wst[wi % 2]
            wi += 1
            k.dma(w_t[:], w_o[l, kc * 128:(kc + 1) * 128, :], [], [w_b])
            k.cp(POOL, wo[:, kc, :], w_t[:], [w_b], [wob])
        rra = {"e": 0, "o": 0, "p": 0}
        OB = (4, 5)

        def tail_stages(qt):
            t0 = qt * 128

            def loads():
                k.dma(sgt[:], SG[:, :, t0:t0 + 128].rearrange("c p t -> p c t"), [dram], [sgtb])
                k.dma(mg1t[:], MG1[:, :, t0:t0 + 128].rearrange("c p t -> p c t"), [dram], [mg1tb])
                k.dma(macct[:], MACC[:, :, t0:t0 + 128].rearrange("c p t -> p c t"), [dram], [macctb])
                k.dma(xtl[:], x_src[:, t0:t0 + 128].rearrange("(c p) t -> p c t", p=128), [dram], [xtlb])

            def st_T():
                bt = 6
                for c in range(4):
                    k.mm(ps[bt][:, c * 128:(c + 1) * 128], on[:, c * 128:(c + 1) * 128], identb, True, True,
                         [onb] + CB, [psb[bt]])
                k.tt(DVE, ogT[:].rearrange("p c q -> p (c q)"), ps[bt][:, :], sgt[:].rearrange("p c q -> p (c q)"),
                     ALU.mult, [psb[bt], sgtb], [ogTb])

            def st_Y():
                for half in range(2):
                    by = 7 if half == 0 else 6
                    for jj in range(4):
                        j = half * 4 + jj
                        for c in range(4):
                            k.mm(ps[by][:, jj * 128:(jj + 1) * 128], woa[:, c, j * 128:(j + 1) * 128], ogT[:, c, :],
                                 c == 0, c == 3, [woab, ogTb], [psb[by]])
                    hs = slice(half * 4, half * 4 + 4)
                    k.tt(DVE, mtmp[:, hs, :].rearrange("p c q -> p (c q)"), ps[by][:, :],
                         mg1t[:, hs, :].rearrange("p c q -> p (c q)"), ALU.mult, [psb[by], mg1tb], [mtmpb])
                k.tt(POOL, mT[:], mtmp[:], macct[:], ALU.add, [mtmpb, macctb], [mTb])

            def st_D():
                for half in range(2):
                    bd = 7 if half == 0 else 6
                    for jj in range(4):
                        jo = half * 4 + jj
                        for j in range(8):
                            k.mm(ps[bd][:, jj * 128:(jj + 1) * 128], wo[:, j, jo * 128:(jo + 1) * 128], mT[:, j, :],
                                 j == 0, j == 7, [wob, mTb], [psb[bd]])
                    hs = slice(half * 4, half * 4 + 4)
                    k.tt(DVE, xtl[:, hs, :].rearrange("p c q -> p (c q)"), ps[bd][:, :],
                         xtl[:, hs, :].rearrange("p c q -> p (c q)"), ALU.add, [psb[bd], xtlb], [xtlb])
                k.dma(x_dst[:, t0:t0 + 128].rearrange("(c p) t -> p c t", p=128), xtl[:], [xtlb], [dram])

            return [loads, st_T, st_Y, st_D]

        pending = []
        for qt in range(NQ):
            t0 = qt * 128
            Lk = t0 + 128
            q_t, q_b = qtt[qt % 2]
            k.dma(q_t[:], QT[:, :, t0:t0 + 128].rearrange("c p t -> p c t"), [dram], [q_b])
            k.dma(mkt[:, 0:Lk], MASKT[qt, :, 0:Lk], [dram], [mktb])
            if pending:
                pending.pop(0)()
            sbank = {}
            ptile = {}

            def emit_S(kt):
                be = rra["e"]
                rra["e"] ^= 1
                bo = 2 + rra["o"]
                rra["o"] ^= 1
                sbank[(kt, 0)] = be
                sbank[(kt, 1)] = bo
                ksl = slice(kt * 128, (kt + 1) * 128)
                for i in range(4):
                    for par, bs in ((0, be), (1, bo)):
                        pb = par * 64
                        k.mm(ps[bs][:, i * 128:(i + 1) * 128], kt2[pb:pb + 64, i, ksl], q_t[pb:pb + 64, i, :], True, True,
                             [kt2b, q_b], [psb[bs]])

            def emit_EM(kt, par):
                bs = sbank[(kt, par)]
                ksl = slice(kt * 128, (kt + 1) * 128)
                p_, p_b = ptl[rra["p"]]
                rra["p"] = (rra["p"] + 1) % NPT
                ptile[(kt, par)] = (p_, p_b)
                k.act(p_[:].rearrange("p h q -> p (h q)"), ps[bs][:, :], AF.Exp, [psb[bs]], [p_b], scale=0.125)
                k.tt(DVE, p_[:], p_[:], mkt[:, ksl].unsqueeze(1).to_broadcast([128, 4, 128]),
                     ALU.mult, [p_b, mktb], [p_b])

            def emit_PV(kt, par):
                p_, p_b = ptile[(kt, par)]
                for i in range(4):
                    h = 2 * i + par
                    k.mm(ps[OB[par]][:, i * 65:(i + 1) * 65], p_[:, i, :], vsb[:, kt, h * 65:(h + 1) * 65],
                         kt == 0 and i == 0, kt == qt, [p_b, vsbb], [psb[OB[par]]], skip=True)

            nk = qt + 1
            emit_S(0)
            for kt in range(nk):
                if kt + 1 < nk:
                    emit_S(kt + 1)
                emit_EM(kt, 0)
                emit_EM(kt, 1)
                emit_PV(kt, 0)
                emit_PV(kt, 1)
                if pending and kt in (0, 1, 2):
                    pending.pop(0)()
            while pending:
                pending.pop(0)()
            for par in range(2):
                ov = ps[OB[par]][:, 0:260].rearrange("p (i e) -> p i e", e=65)
                k.recip(rden[:, par * 4:(par + 1) * 4], ov[:, :, 64], [psb[OB[par]]], [rdenb])
                k.tt(DVE, on[:].rearrange("p (i g d) -> p i g d", g=2, d=64)[:, :, par, :], ov[:, :, 0:64],
                     rden[:, par * 4:(par + 1) * 4].unsqueeze(2).to_broadcast([128, 4, 64]), ALU.mult,
                     [psb[OB[par]], rdenb], [onb])
            pending = tail_stages(qt)
        while pending:
            pending.pop(0)()
        S_.barrier()

    for l in range(L):
        x_src = xT_in if l == 0 else xbuf
        x_dst = outT if l == L - 1 else xbuf
        if "P" in phases:
            phase_P(l, x_src)
        if "I" in phases:
            phase_I(l)
        if "A" in phases:
            phase_A(l, x_src, x_dst)
    S_.barrier()
    S_.emit(nc, stack)
    stack.close()
    return nc, S_


_CACHE = {}


def _get_program(S, L):
    key = (S, L)
    if key not in _CACHE:
        _CACHE[key] = build_program(S, L)[0]
    return _CACHE[key]


def run_layers(x, norm_g, w_in, conv_w, w_out_conv, q_norm_g, k_norm_g, w_out_attn, pool_w, pool_scale,
               w_out_pool, w_o, L=None, nc=None):
    B, S, _ = x.shape
    if L is None:
        L = norm_g.shape[0]
    cosT, sinT, cmat = host_constants(S)
    vecs = host_vectors(norm_g[:L], conv_w[:L], q_norm_g[:L], k_norm_g[:L], pool_scale[:L])
    if nc is None:
        nc = _get_program(S, L)
    shared = {
        "w_in": np.ascontiguousarray(w_in[:L], dtype=np.float32),
        "w_out_conv": np.ascontiguousarray(w_out_conv[:L], dtype=np.float32),
        "w_out_attn": np.ascontiguousarray(w_out_attn[:L], dtype=np.float32),
        "w_out_pool": np.ascontiguousarray(w_out_pool[:L], dtype=np.float32),
        "w_o": np.ascontiguousarray(w_o[:L], dtype=np.float32),
        "pool_w": np.ascontiguousarray(pool_w[:L], dtype=np.float32),
        "vecs": vecs, "cosT": cosT, "sinT": sinT, "cmat": cmat,
    }
    zeros = {k_: np.zeros_like(v_) for k_, v_ in shared.items()}
    in_maps = []
    for c in range(8):
        if c < B:
            m = dict(shared)
            m["xT"] = np.ascontiguousarray(x[c].T, dtype=np.float32)
        else:
            m = dict(zeros)
            m["xT"] = np.zeros((D, S), np.float32)
        in_maps.append(m)
    res = run_bass_kernel_spmd(nc, in_maps, core_ids=list(range(8)))
    out = np.stack([np.ascontiguousarray(res.results[b]["outT"].T) for b in range(B)], axis=0)
    return out.astype(np.float32), res


def kernel(x, norm_g, w_in, conv_w, w_out_conv, q_norm_g, k_norm_g, w_out_attn, pool_w, pool_scale, w_out_pool, w_o):
    args = [np.asarray(a, dtype=np.float32) for a in
            (x, norm_g, w_in, conv_w, w_out_conv, q_norm_g, k_norm_g, w_out_attn, pool_w, pool_scale, w_out_pool, w_o)]
    out, _ = run_layers(*args)
    return out
```

```python
import math
from contextlib import ExitStack

import numpy as np
import ml_dtypes

import concourse.bass as bass
import concourse.mybir as mybir
from concourse.bass_utils import run_bass_kernel_spmd

F32 = mybir.dt.float32
BF16 = mybir.dt.bfloat16
ALU = mybir.AluOpType
AF = mybir.ActivationFunctionType
AX = mybir.AxisListType

PE, ACT, DVE, POOL, SP = "pe", "act", "dve", "pool", "sp"
COMPUTE = (PE, ACT, DVE, POOL)
N_LANES = 32

D = 1024
DEPTH = 4
SEQ = 8192
BATCH = 4
INW = 8776
C_CB, C_CC, C_CX, C_CG = 0, 512, 1024, 1536
C_Q, C_K, C_V, C_AG = 2048, 2560, 3072, 3584
C_IQ, C_IK, C_IW = 4096, 4608, 4672
C_PU, C_PG = 4680, 5192
C_MG = 5704
TOPK = 256
RMS_EPS = 1e-6
POOL_WINDOWS = (2, 4, 8, 16)
NITER = 20
NEG = -1.0e30


class Buf:
    __slots__ = ("name", "last_w", "readers")

    def __init__(self, name):
        self.name = name
        self.last_w = None
        self.readers = {}


class Sched:
    def __init__(self):
        self.ops = {e: [] for e in (PE, ACT, DVE, POOL, SP)}
        self.cnt = {e: 0 for e in COMPUTE}
        self.waited = {e: {} for e in (PE, ACT, DVE, POOL, SP)}
        self.lane_cnt = [0] * N_LANES
        self.next_lane = 0
        self.n_ops = 0

    def _deps(self, reads, writes):
        deps = {}
        reads = [b for b in reads if b.name != "dram"]
        writes = [b for b in writes if b.name != "dram"]
        for b in reads:
            t = b.last_w
            if t is not None and deps.get(t[0], 0) < t[1]:
                deps[t[0]] = t[1]
        for b in writes:
            t = b.last_w
            if t is not None and deps.get(t[0], 0) < t[1]:
                deps[t[0]] = t[1]
            for k, v in b.readers.items():
                if deps.get(k, 0) < v:
                    deps[k] = v
        return deps

    def _commit(self, tok, reads, writes):
        k, v = tok
        reads = [b for b in reads if b.name != "dram"]
        writes = [b for b in writes if b.name != "dram"]
        for b in writes:
            b.last_w = tok
            b.readers = {}
        for b in reads:
            if b.readers.get(k, 0) < v:
                b.readers[k] = v

    def _filter(self, eng, deps):
        waits = []
        wd = self.waited[eng]
        for k, v in deps.items():
            if k == eng:
                if eng == PE:
                    continue
                if v <= self.cnt[eng] - 3:
                    continue
            if wd.get(k, 0) >= v:
                continue
            wd[k] = v
            waits.append((k, v))
        return waits

    def op(self, eng, fn, reads=(), writes=()):
        deps = self._deps(reads, writes)
        waits = self._filter(eng, deps)
        self.cnt[eng] += 1
        tok = (eng, self.cnt[eng])
        self.ops[eng].append((waits, fn, (eng, 1)))
        self._commit(tok, reads, writes)
        self.n_ops += 1
        return tok

    def dma(self, fn, reads=(), writes=(), queue=SP):
        deps = self._deps(reads, writes)
        lane = self.next_lane
        self.next_lane = (self.next_lane + 1) % N_LANES
        lk = ("lane", lane)
        if self.lane_cnt[lane] > 0 and deps.get(lk, 0) < self.lane_cnt[lane]:
            deps[lk] = self.lane_cnt[lane]
        waits = self._filter(queue, deps)
        self.lane_cnt[lane] += 16
        tok = (lk, self.lane_cnt[lane])
        self.ops[queue].append((waits, fn, (lk, 16)))
        self._commit(tok, reads, writes)
        self.n_ops += 1
        return tok

    def barrier(self):
        targets = {}
        for e in COMPUTE:
            if self.cnt[e] > 0:
                targets[e] = self.cnt[e]
        for i, c in enumerate(self.lane_cnt):
            if c > 0:
                targets[("lane", i)] = c
        for e in (PE, ACT, DVE, POOL, SP):
            waits = []
            for k, v in targets.items():
                if k == e or self.waited[e].get(k, 0) >= v:
                    continue
                self.waited[e][k] = v
                waits.append((k, v))
            if waits:
                self.ops[e].append((waits, None, None))

    def emit(self, nc, stack):
        sems = {}
        for e in COMPUTE:
            sems[e] = stack.enter_context(nc.semaphore("s_" + e))
        for i in range(N_LANES):
            sems[("lane", i)] = stack.enter_context(nc.semaphore("s_lane%d" % i))
        block = stack.enter_context(nc.Block())
        ops = self.ops

        def run(name):
            def body(eng):
                for waits, fn, inc in ops[name]:
                    for k, v in waits:
                        eng.wait_ge(sems[k], v)
                    if fn is not None:
                        fn(eng).then_inc(sems[inc[0]], inc[1])
            return body

        block.sync(run(SP))
        block.tensor(run(PE))
        block.scalar(run(ACT))
        block.vector(run(DVE))
        block.gpsimd(run(POOL))


class Arena:
    def __init__(self, nc, base=16384, limit=229376 - 256):
        self.nc, self.base, self.off, self.limit, self.n = nc, base, base, limit, 0

    def alloc(self, name, shape, dtype):
        esz = 4 if dtype == F32 else 2
        n = 1
        for s in shape[1:]:
            n *= s
        nbytes = (n * esz + 63) // 64 * 64
        off = self.off
        assert off + nbytes <= self.limit, ("SBUF overflow", name, off, nbytes, self.limit)
        self.off += nbytes
        self.n += 1
        t = self.nc.alloc_sbuf_tensor_at("%s_%d" % (name, self.n), list(shape), dtype, offset=off)
        return t, Buf(name)


class K:
    def __init__(self, S):
        self.S = S

    def mm(self, out, lhsT, rhs, start, stop, reads, writes, skip=False):
        if skip:
            self.S.op(PE, lambda e: e.matmul(out, lhsT=lhsT, rhs=rhs, start=start, stop=stop, skip_group_check=True),
                      reads, writes)
        else:
            self.S.op(PE, lambda e: e.matmul(out, lhsT=lhsT, rhs=rhs, start=start, stop=stop), reads, writes)

    def act(self, out, in_, func, reads, writes, scale=1.0, bias=None):
        if bias is None:
            self.S.op(ACT, lambda e: e.activation(out=out, in_=in_, func=func, scale=scale), reads, writes)
        else:
            self.S.op(ACT, lambda e: e.activation(out=out, in_=in_, func=func, scale=scale, bias=bias), reads, writes)

    def tt(self, eng, out, in0, in1, op, reads, writes):
        self.S.op(eng, lambda e: e.tensor_tensor(out=out, in0=in0, in1=in1, op=op), reads, writes)

    def ts(self, eng, out, in0, s1, s2, op0, op1, reads, writes, accum_out=None):
        if op1 is None:
            self.S.op(eng, lambda e: e.tensor_scalar(out=out, in0=in0, scalar1=s1, scalar2=None, op0=op0), reads, writes)
        elif accum_out is None:
            self.S.op(eng, lambda e: e.tensor_scalar(out=out, in0=in0, scalar1=s1, scalar2=s2, op0=op0, op1=op1), reads, writes)
        else:
            self.S.op(eng, lambda e: e.tensor_scalar(out=out, in0=in0, scalar1=s1, scalar2=s2, op0=op0, op1=op1,
                                                     accum_out=accum_out), reads, writes)

    def stt(self, eng, out, in0, scalar, in1, op0, op1, reads, writes):
        self.S.op(eng, lambda e: e.scalar_tensor_tensor(out=out, in0=in0, scalar=scalar, in1=in1, op0=op0, op1=op1),
                  reads, writes)

    def cp(self, eng, out, in_, reads, writes):
        self.S.op(eng, lambda e: e.tensor_copy(out=out, in_=in_), reads, writes)

    def memset(self, eng, ap, val, writes):
        self.S.op(eng, lambda e: e.memset(ap, val), (), writes)

    def recip(self, out, in_, reads, writes):
        self.S.op(DVE, lambda e: e.reciprocal(out=out, in_=in_), reads, writes)

    def reduce(self, out, in_, op, reads, writes):
        self.S.op(DVE, lambda e: e.tensor_reduce(out=out, in_=in_, axis=AX.X, op=op), reads, writes)

    def dma(self, out, in_, reads, writes, queue=SP):
        self.S.dma(lambda e: e.dma_start(out=out, in_=in_), reads, writes, queue=queue)


def host_constants(S):
    f32 = np.float32
    inv = (1.0 / (np.float32(10000.0) ** (np.arange(0, 64, 2, dtype=f32) / f32(64)))).astype(f32)
    ang = (np.arange(S, dtype=f32)[:, None] * inv[None, :]).astype(f32)
    cos = np.cos(ang).astype(f32).T
    sin = np.sin(ang).astype(f32).T
    cosT = np.ascontiguousarray(np.tile(cos, (4, 1)))
    sinT = np.ascontiguousarray(np.tile(sin, (4, 1)))
    ident = np.eye(128, dtype=f32)
    rot = np.zeros((128, 128), f32)
    for m in range(128):
        if (m % 64) < 32:
            rot[m + 32, m] = -1.0
        else:
            rot[m - 32, m] = 1.0
    bones = np.zeros((128, 128), f32)
    bones[0:64, 0:64] = 1.0
    bones[64:128, 64:128] = 1.0
    ones = np.ones((128, 128), f32)
    negm = np.where(np.arange(128)[None, :] <= np.arange(128)[:, None], 0.0, NEG).astype(f32)
    corr = np.ones((128, 4, 16), f32)
    for gi, w in enumerate(POOL_WINDOWS):
        t = np.arange(16)
        corr[:, gi, :] = (w / np.minimum(t + 1, w)).astype(f32)[None, :]
    cmat = np.concatenate([ident, rot, bones, ones, negm, corr.reshape(128, 64)], axis=1)
    return cosT, sinT, np.ascontiguousarray(cmat)


def host_vectors(norm_g, conv_w, q_norm_g, k_norm_g, pool_scale):
    L = norm_g.shape[0]
    ng = norm_g.reshape(L, 8, 128).transpose(2, 0, 1).reshape(128, L * 8)
    cw = conv_w.reshape(L, 3, 4, 128).transpose(3, 0, 1, 2).reshape(128, L * 12)
    qg = np.tile(q_norm_g.T, (2, 1))
    kg = np.tile(k_norm_g.T, (2, 1))
    psc = pool_scale.reshape(L, 4, 128).transpose(2, 0, 1).reshape(128, L * 4)
    return np.ascontiguousarray(np.concatenate([ng, cw, qg, kg, psc], axis=1).astype(np.float32))


def build_program(S, L, debug=False, phases="PIA"):
    NQ = S // 128
    TS = 2048 if S % 2048 == 0 else S
    NST = S // TS
    NSUB = TS // 512
    topk = min(TOPK, S // 4)
    nc = bass.Bass("TRN2", target_bir_lowering=False)

    def din(name, shape, dt=F32):
        return nc.dram_tensor(name, list(shape), dt, kind="ExternalInput").ap()

    dbg_kind = "ExternalOutput" if debug else "Internal"

    def dscr(name, shape, dt):
        return nc.dram_tensor(name, list(shape), dt, kind=dbg_kind).ap()

    xT_in = din("xT", [D, S])
    w_in = din("w_in", [L, D, INW])
    w_oc = din("w_out_conv", [L, 512, D])
    w_oa = din("w_out_attn", [L, 512, D])
    w_op = din("w_out_pool", [L, 512, D])
    w_o = din("w_o", [L, D, D])
    pool_w = din("pool_w", [L, 4, 128, 128])
    vecs = din("vecs", [128, L * 26])
    cosT = din("cosT", [128, S])
    sinT = din("sinT", [128, S])
    cmat = din("cmat", [128, 704])
    outT = nc.dram_tensor("outT", [D, S], F32, kind="ExternalOutput").ap()

    xbuf = nc.dram_tensor("xbuf", [D, S], F32, kind="Internal").ap()
    QT = dscr("QT", [4, 128, S], BF16)
    KT = dscr("KT", [4, 128, S], BF16)
    IQT = dscr("IQT", [4, 128, S], BF16)
    IKT = dscr("IKT", [64, S], BF16)
    VA = dscr("VA", [S, 520], BF16)
    IW = dscr("IW", [S, 8], F32)
    SG = dscr("SG", [4, 128, S], BF16)
    MG1 = dscr("MG1", [8, 128, S], BF16)
    MACC = dscr("MACC", [8, 128, S], F32)
    MASKT = dscr("MASKT", [NQ, 128, S], BF16)

    S_ = Sched()
    k = K(S_)
    ar = Arena(nc)
    stack = ExitStack()
    ps = [stack.enter_context(nc.psum_tensor("ps%d" % i, [128, 512], F32)) for i in range(8)]
    psb = [Buf("ps%d" % i) for i in range(8)]
    dram = Buf("dram")

    cst_f, cst_fb = ar.alloc("cst_f", [128, 704], F32)
    cst_b, cst_bb = ar.alloc("cst_b", [128, 512], BF16)
    vec, vecb = ar.alloc("vec", [128, L * 26], F32)
    eps_t, eps_b = ar.alloc("eps", [128, 2], F32)
    k.dma(cst_f[:], cmat, [], [cst_fb])
    k.dma(vec[:], vecs, [], [vecb])
    k.cp(DVE, cst_b[:], cst_f[:, 0:512], [cst_fb], [cst_bb])
    k.memset(POOL, eps_t[:, 0:1], RMS_EPS, [eps_b])
    k.memset(POOL, eps_t[:, 1:2], -1.0e29, [eps_b])
    identb = cst_b[:, 0:128]
    rotb = cst_b[:, 128:256]
    bonesb = cst_b[:, 256:384]
    onesb = cst_b[:, 384:512]
    negm = cst_f[:, 512:640]
    corr = cst_f[:, 640:704]
    CB = [cst_bb, cst_fb, vecb, eps_b]

    def v_ng(l, kc):
        return vec[:, l * 8 + kc: l * 8 + kc + 1]

    def v_cw(l, j, i):
        o = L * 8 + l * 12 + j * 4 + i
        return vec[:, o:o + 1]

    def v_qg(l):
        o = L * 20 + l
        return vec[:, o:o + 1]

    def v_kg(l):
        o = L * 21 + l
        return vec[:, o:o + 1]

    def v_psc(l, i):
        o = L * 22 + l * 4 + i
        return vec[:, o:o + 1]

    base_off = ar.off
    bank_rr = [0]

    def nbank():
        b = bank_rr[0]
        bank_rr[0] = (b + 1) % 8
        return b

    def phase_P(l, x_src):
        ar.off = base_off
        hT, hTb = ar.alloc("hT", [128, 8, TS], BF16)
        yA, yAb = ar.alloc("yA", [128, 4, TS], BF16)
        yC, yCb = ar.alloc("yC", [128, 4, TS], BF16)
        xt, xtb = ar.alloc("xt", [128, 8, 512], F32)
        sq, sqb = ar.alloc("sq", [128, 8, 512], BF16)
        rstd, rstdb = ar.alloc("rstd", [128, 512], F32)
        cs, csb = ar.alloc("cs", [128, TS], F32)
        sn, snb = ar.alloc("sn", [128, TS], F32)
        NWS = 4
        wst = [ar.alloc("wst%d" % i, [128, 8, 128], F32) for i in range(NWS)]
        NWB = 8
        wbf_all, _ = ar.alloc("wbf", [128, NWB, 8, 128], BF16)
        wbfb = [Buf("wbf%d" % i) for i in range(NWB)]
        zbuf, zb = ar.alloc("zbuf", [128, 4, 514], F32)
        ubuf, ub = ar.alloc("ubuf", [128, 4, 528], F32)
        NT = 6
        tf = [ar.alloc("tf%d" % i, [128, 528], F32) for i in range(NT)]
        NTB = 6
        tb = [ar.alloc("tb%d" % i, [128, 512], BF16) for i in range(NTB)]
        vst = [ar.alloc("vst%d" % i, [128, 8, 65], BF16) for i in range(2)]
        iwst, iwstb = ar.alloc("iwst", [128, TS // 128, 8], F32)
        rr = {"ws": 0, "wb": 0, "tf": 0, "tb": 0}

        def ntf():
            i = rr["tf"]
            rr["tf"] = (i + 1) % NT
            return tf[i]

        def ntb():
            i = rr["tb"]
            rr["tb"] = (i + 1) % NTB
            return tb[i]

        def load_w(src, kcn, ncols):
            si = rr["ws"]
            rr["ws"] = (si + 1) % NWS
            bi = rr["wb"]
            rr["wb"] = (bi + 1) % NWB
            st_t, st_b = wst[si]
            k.dma(st_t[:, 0:kcn, 0:ncols], src.rearrange("(kc p) c -> p kc c", p=128), [], [st_b])
            k.cp(POOL, wbf_all[:, bi, 0:kcn, 0:ncols], st_t[:, 0:kcn, 0:ncols], [st_b], [wbfb[bi]])
            return bi

        def proj(bi, sub, ncols=128, bank=None):
            b = nbank() if bank is None else bank
            for kc in range(8):
                k.mm(ps[b][0:ncols, :], wbf_all[:, bi, kc, 0:ncols], hT[:, kc, sub * 512:(sub + 1) * 512],
                     kc == 0, kc == 7, [wbfb[bi], hTb], [psb[b]])
            return b

        k.memset(POOL, zbuf[:], 0.0, [zb])
        k.memset(POOL, ubuf[:], 0.0, [ub])
        for i in range(2):
            k.memset(POOL, vst[i][0][:], 1.0, [vst[i][1]])

        for st in range(NST):
            T0 = st * TS
            for sub in range(NSUB):
                t0 = T0 + sub * 512
                k.dma(xt[:], x_src[:, t0:t0 + 512].rearrange("(kc p) t -> p kc t", p=128), [dram], [xtb])
                for kc in range(8):
                    k.act(sq[:, kc, :], xt[:, kc, :], AF.Square, [xtb], [sqb])
                b = nbank()
                for kc in range(8):
                    k.mm(ps[b][:, :], onesb, sq[:, kc, :], kc == 0, kc == 7, [sqb] + CB, [psb[b]])
                k.act(rstd[:], ps[b][:, :], AF.Sqrt, [psb[b]] + CB, [rstdb], scale=1.0 / D, bias=eps_t[:, 0:1])
                k.recip(rstd[:], rstd[:], [rstdb], [rstdb])
                for kc in range(8):
                    k.stt(DVE, hT[:, kc, sub * 512:(sub + 1) * 512], xt[:, kc, :], v_ng(l, kc), rstd[:],
                          ALU.mult, ALU.mult, [xtb, rstdb] + CB, [hTb])
            k.dma(cs[:], cosT[:, T0:T0 + TS], [], [csb])
            k.dma(sn[:], sinT[:, T0:T0 + TS], [], [snb])

            for i in range(4):
                wb_ = [load_w(w_in[l, :, c0 + i * 128: c0 + (i + 1) * 128], 8, 128) for c0 in (C_CB, C_CC, C_CX, C_CG)]
                for sub in range(NSUB):
                    b_cb, b_cc, b_cx, b_cg = [proj(w, sub) for w in wb_]
                    t_cc, t_ccb = ntf()
                    k.act(t_cc[:, 0:512], ps[b_cc][:, :], AF.Copy, [psb[b_cc]], [t_ccb])
                    k.tt(DVE, zbuf[:, i, 2:514], ps[b_cx][:, :], t_cc[:, 0:512], ALU.mult, [psb[b_cx], t_ccb], [zb])
                    acc, accb = ntf()
                    k.ts(DVE, acc[:, 0:512], zbuf[:, i, 0:512], v_cw(l, 0, i), None, ALU.mult, None, [zb] + CB, [accb])
                    k.stt(DVE, acc[:, 0:512], zbuf[:, i, 1:513], v_cw(l, 1, i), acc[:, 0:512], ALU.mult, ALU.add,
                          [zb, accb] + CB, [accb])
                    k.stt(DVE, acc[:, 0:512], zbuf[:, i, 2:514], v_cw(l, 2, i), acc[:, 0:512], ALU.mult, ALU.add,
                          [zb, accb] + CB, [accb])
                    k.cp(POOL, zbuf[:, i, 0:2], zbuf[:, i, 512:514], [zb], [zb])
                    sg, sgb = ntf()
                    k.act(sg[:, 0:512], ps[b_cg][:, :], AF.Sigmoid, [psb[b_cg]], [sgb])
                    k.tt(DVE, sg[:, 0:512], ps[b_cg][:, :], sg[:, 0:512], ALU.mult, [psb[b_cg], sgb], [sgb])
                    k.tt(DVE, sg[:, 0:512], ps[b_cb][:, :], sg[:, 0:512], ALU.mult, [psb[b_cb], sgb], [sgb])
                    k.tt(POOL, yA[:, i, sub * 512:(sub + 1) * 512], sg[:, 0:512], acc[:, 0:512], ALU.mult,
                         [sgb, accb], [yAb])

            for i in range(4):
                w = POOL_WINDOWS[i]
                w_pu = load_w(w_in[l, :, C_PU + i * 128: C_PU + (i + 1) * 128], 8, 128)
                w_pg = load_w(w_in[l, :, C_PG + i * 128: C_PG + (i + 1) * 128], 8, 128)
                w_pw = load_w(pool_w[l, i], 1, 128)
                for sub in range(NSUB):
                    b_pu = proj(w_pu, sub)
                    b_pg = proj(w_pg, sub)
                    k.act(ubuf[:, i, 16:528], ps[b_pu][:, :], AF.Copy, [psb[b_pu]], [ub])
                    cur, curb = None, None
                    src, srcb = ubuf[:, i, :], ub
                    sh = 1
                    while sh < w:
                        nt_, ntb_ = ntf()
                        k.tt(POOL, nt_[:, sh:528], src[:, sh:528], src[:, 0:528 - sh], ALU.add, [srcb], [ntb_])
                        if sh == 1:
                            pass
                        src, srcb = nt_, ntb_
                        sh *= 2
                    first = (st == 0 and sub == 0)
                    pooled, pooledb = ntb()
                    if first:
                        m_, m_b = ntf()
                        k.ts(DVE, m_[:, 0:512], src[:, 16:528], 1.0 / w, None, ALU.mult, None, [srcb], [m_b])
                        k.tt(DVE, m_[:, 0:16], m_[:, 0:16], corr[:, i * 16:(i + 1) * 16], ALU.mult, [m_b] + CB, [m_b])
                        k.tt(DVE, pooled[:], m_[:, 0:512], ubuf[:, i, 16:528], ALU.subtract, [m_b, ub], [pooledb])
                    else:
                        k.stt(DVE, pooled[:], src[:, 16:528], 1.0 / w, ubuf[:, i, 16:528], ALU.mult, ALU.subtract,
                              [srcb, ub], [pooledb])
                    k.cp(POOL, ubuf[:, i, 0:16], ubuf[:, i, 512:528], [ub, srcb], [ub])
                    b_m = nbank()
                    k.mm(ps[b_m][:, :], wbf_all[:, w_pw, 0, :], pooled[:], True, True, [wbfb[w_pw], pooledb], [psb[b_m]])
                    sg, sgb = ntf()
                    k.act(sg[:, 0:512], ps[b_pg][:, :], AF.Sigmoid, [psb[b_pg]], [sgb])
                    k.tt(DVE, sg[:, 0:512], ps[b_pg][:, :], sg[:, 0:512], ALU.mult, [psb[b_pg], sgb], [sgb])
                    k.stt(DVE, yC[:, i, sub * 512:(sub + 1) * 512], ps[b_m][:, :], v_psc(l, i), sg[:, 0:512],
                          ALU.mult, ALU.mult, [psb[b_m], sgb] + CB, [yCb])

            for j in range(8):
                w_a = load_w(w_oc[l, :, j * 128:(j + 1) * 128], 4, 128)
                w_c = load_w(w_op[l, :, j * 128:(j + 1) * 128], 4, 128)
                w_g0 = load_w(w_in[l, :, C_MG + j * 128: C_MG + (j + 1) * 128], 8, 128)
                w_g2 = load_w(w_in[l, :, C_MG + 2048 + j * 128: C_MG + 2048 + (j + 1) * 128], 8, 128)
                for sub in range(NSUB):
                    sl = slice(sub * 512, (sub + 1) * 512)
                    bA = nbank()
                    for kc in range(4):
                        k.mm(ps[bA][:, :], wbf_all[:, w_a, kc, :], yA[:, kc, sl], kc == 0, kc == 3, [wbfb[w_a], yAb], [psb[bA]])
                    bG0 = proj(w_g0, sub)
                    bC = nbank()
                    for kc in range(4):
                        k.mm(ps[bC][:, :], wbf_all[:, w_c, kc, :], yC[:, kc, sl], kc == 0, kc == 3, [wbfb[w_c], yCb], [psb[bC]])
                    bG2 = proj(w_g2, sub)
                    sA, sAb = ntf()
                    sC, sCb = ntf()
                    k.act(sA[:, 0:512], ps[bG0][:, :], AF.Sigmoid, [psb[bG0]], [sAb])
                    k.act(sC[:, 0:512], ps[bG2][:, :], AF.Sigmoid, [psb[bG2]], [sCb])
                    k.tt(DVE, sA[:, 0:512], ps[bA][:, :], sA[:, 0:512], ALU.mult, [psb[bA], sAb], [sAb])
                    k.tt(DVE, sC[:, 0:512], ps[bC][:, :], sC[:, 0:512], ALU.mult, [psb[bC], sCb], [sCb])
                    k.tt(POOL, sA[:, 0:512], sA[:, 0:512], sC[:, 0:512], ALU.add, [sAb, sCb], [sAb])
                    t0 = T0 + sub * 512
                    k.dma(MACC[j, :, t0:t0 + 512], sA[:, 0:512], [sAb], [dram])

            for (c0, dst, gfun) in ((C_Q, QT, v_qg), (C_K, KT, v_kg)):
                for c in range(4):
                    wq = load_w(w_in[l, :, c0 + c * 128: c0 + (c + 1) * 128], 8, 128)
                    for sub in range(NSUB):
                        sl = slice(sub * 512, (sub + 1) * 512)
                        b = proj(wq, sub)
                        qg_, qgb = ntb()
                        sq_, sq_b = ntb()
                        k.act(qg_[:], ps[b][:, :], AF.Copy, [psb[b]] + CB, [qgb], scale=gfun(l))
                        k.act(sq_[:], ps[b][:, :], AF.Square, [psb[b]], [sq_b])
                        bS = nbank()
                        k.mm(ps[bS][:, :], bonesb, sq_[:], True, True, [sq_b] + CB, [psb[bS]])
                        bR = nbank()
                        k.mm(ps[bR][:, :], rotb, qg_[:], True, True, [qgb] + CB, [psb[bR]])
                        t1, t1b = ntf()
                        t2, t2b = ntf()
                        k.tt(DVE, t1[:, 0:512], qg_[:], cs[:, sl], ALU.mult, [qgb, csb], [t1b])
                        k.tt(DVE, t2[:, 0:512], ps[bR][:, :], sn[:, sl], ALU.mult, [psb[bR], snb], [t2b])
                        k.tt(POOL, t1[:, 0:512], t1[:, 0:512], t2[:, 0:512], ALU.add, [t1b, t2b], [t1b])
                        k.act(t2[:, 0:512], ps[bS][:, :], AF.Sqrt, [psb[bS], t2b] + CB, [t2b], scale=1.0 / 64,
                              bias=eps_t[:, 0:1])
                        k.recip(t2[:, 0:512], t2[:, 0:512], [t2b], [t2b])
                        o_, ob = ntb()
                        k.tt(DVE, o_[:], t1[:, 0:512], t2[:, 0:512], ALU.mult, [t1b, t2b], [ob])
                        t0 = T0 + sub * 512
                        k.dma(dst[c, :, t0:t0 + 512], o_[:], [ob], [dram])

            for c in range(5):
                ncols = 128 if c < 4 else 64
                col = C_IQ + c * 128 if c < 4 else C_IK
                wq = load_w(w_in[l, :, col: col + ncols], 8, ncols)
                for sub in range(NSUB):
                    sl = slice(sub * 512, (sub + 1) * 512)
                    b = proj(wq, sub, ncols)
                    b_, bb = ntb()
                    k.act(b_[0:ncols, :], ps[b][0:ncols, :], AF.Copy, [psb[b]], [bb])
                    bR = nbank()
                    k.mm(ps[bR][0:ncols, :], cst_b[0:ncols, 128:128 + ncols], b_[0:ncols, :], True, True, [bb] + CB, [psb[bR]])
                    t1, t1b = ntf()
                    t2, t2b = ntf()
                    k.tt(DVE, t1[0:ncols, 0:512], b_[0:ncols, :], cs[0:ncols, sl], ALU.mult, [bb, csb], [t1b])
                    k.tt(DVE, t2[0:ncols, 0:512], ps[bR][0:ncols, :], sn[0:ncols, sl], ALU.mult, [psb[bR], snb], [t2b])
                    o_, ob = ntb()
                    k.tt(POOL, o_[0:ncols, :], t1[0:ncols, 0:512], t2[0:ncols, 0:512], ALU.add, [t1b, t2b], [ob])
                    t0 = T0 + sub * 512
                    if c < 4:
                        k.dma(IQT[c, :, t0:t0 + 512], o_[:], [ob], [dram])
                    else:
                        k.dma(IKT[:, t0:t0 + 512], o_[0:64, :], [ob], [dram])

            for c in range(4):
                wq = load_w(w_in[l, :, C_AG + c * 128: C_AG + (c + 1) * 128], 8, 128)
                for sub in range(NSUB):
                    b = proj(wq, sub)
                    sg, sgb = ntf()
                    k.act(sg[:, 0:512], ps[b][:, :], AF.Sigmoid, [psb[b]], [sgb])
                    o_, ob = ntb()
                    k.tt(DVE, o_[:], ps[b][:, :], sg[:, 0:512], ALU.mult, [psb[b], sgb], [ob])
                    t0 = T0 + sub * 512
                    k.dma(SG[c, :, t0:t0 + 512], o_[:], [ob], [dram])
            for j in range(8):
                wq = load_w(w_in[l, :, C_MG + 1024 + j * 128: C_MG + 1024 + (j + 1) * 128], 8, 128)
                for sub in range(NSUB):
                    b = proj(wq, sub)
                    o_, ob = ntb()
                    k.act(o_[:], ps[b][:, :], AF.Sigmoid, [psb[b]], [ob])
                    t0 = T0 + sub * 512
                    k.dma(MG1[j, :, t0:t0 + 512], o_[:], [ob], [dram])

            wv = [load_w(w_in[l, :, C_V + c * 128: C_V + (c + 1) * 128], 8, 128) for c in range(4)]
            wiw = load_w(w_in[l, :, C_IW: C_IW + 8], 8, 8)
            for tt_ in range(TS // 128):
                tsl = slice(tt_ * 128, (tt_ + 1) * 128)
                vt, vtb = vst[tt_ % 2]
                for c in range(4):
                    b = nbank()
                    for kc in range(8):
                        k.mm(ps[b][:, 0:128], hT[:, kc, tsl], wbf_all[:, wv[c], kc, :], kc == 0, kc == 7,
                             [hTb, wbfb[wv[c]]], [psb[b]])
                    k.act(vt[:, 2 * c:2 * c + 2, 0:64], ps[b][:, 0:128].rearrange("p (h d) -> p h d", h=2), AF.Copy,
                          [psb[b]], [vtb])
                tok = T0 + tt_ * 128
                k.dma(VA[tok:tok + 128, :], vt[:].rearrange("p h d -> p (h d)"), [vtb], [dram])
                b = nbank()
                for kc in range(8):
                    k.mm(ps[b][:, 0:8], hT[:, kc, tsl], wbf_all[:, wiw, kc, 0:8], kc == 0, kc == 7,
                         [hTb, wbfb[wiw]], [psb[b]])
                k.cp(DVE, iwst[:, tt_, :], ps[b][:, 0:8], [psb[b]], [iwstb])
            k.dma(IW[T0:T0 + TS, :].rearrange("(t p) h -> p t h", p=128), iwst[:], [iwstb], [dram])
        S_.barrier()

    def phase_I(l):
        ar.off = base_off
        G = 3
        ALPHA = 0.0
        ik2, ik2b = ar.alloc("ik2", [128, S], BF16)
        scores = [ar.alloc("score%d" % i, [128, S], F32) for i in range(G)]
        masks = [ar.alloc("mask%d" % i, [128, S], BF16) for i in range(G)]
        iqt = [ar.alloc("iqt%d" % i, [128, 4, 128], BF16) for i in range(2)]
        iwt = [ar.alloc("iwt%d" % i, [128, 8], F32) for i in range(2)]
        diag = [ar.alloc("diag%d" % i, [128, 8, 128], BF16) for i in range(2)]
        NR = 6
        rt = [ar.alloc("rt%d" % i, [128, 512], BF16) for i in range(NR)]
        mts = [ar.alloc("mts%d" % i, [128, 512], BF16) for i in range(3)]
        st, _ = ar.alloc("bst", [128, 8, G], F32)
        b_lw, b_mid, b_cnt, b_ca, b_thr = Buf("lw"), Buf("mid"), Buf("cnt"), Buf("ca"), Buf("thr")
        b_mA = [Buf("mA%d" % i) for i in range(G)]
        b_mD = [Buf("mD%d" % i) for i in range(G)]
        LO, WD, MID, CNT, CA, TOT, SW, THR = range(8)
        rri = {"r": 0, "m": 0, "e": 0, "o": 0, "s": 0, "t": 0}
        k.dma(ik2[0:64, :], IKT, [dram], [ik2b])
        k.dma(ik2[64:128, :], IKT, [dram], [ik2b])
        k.memset(POOL, st[:], 0.0, [b_lw, b_mid, b_cnt, b_ca, b_thr])

        def indexer(qt, score, scoreb):
            t0 = qt * 128
            Lk = t0 + 128
            iq_t, iq_b = iqt[qt % 2]
            iw_t, iw_b = iwt[qt % 2]
            dg, dgb = diag[qt % 2]
            k.dma(iq_t[:], IQT[:, :, t0:t0 + 128].rearrange("c p t -> p c t"), [dram], [iq_b])
            k.dma(iw_t[:], IW[t0:t0 + 128, :], [dram], [iw_b])
            k.ts(POOL, iw_t[:], iw_t[:], 0.125 * (8.0 ** -0.5), None, ALU.mult, None, [iw_b], [iw_b])
            for h in range(8):
                k.act(dg[:, h, :], identb, AF.Copy, [iw_b] + CB, [dgb], scale=iw_t[:, h:h + 1])
            for kc in range((Lk + 511) // 512):
                n = min(512, Lk - kc * 512)
                ksl = slice(kc * 512, kc * 512 + n)
                bsc = 4 + rri["s"]
                rri["s"] ^= 1
                LAG = 2
                rts = {}
                for step in range(8 + LAG):
                    if step < 8:
                        h = step
                        par = h % 2
                        if par == 0:
                            bx = rri["e"]
                            rri["e"] ^= 1
                        else:
                            bx = 2 + rri["o"]
                            rri["o"] ^= 1
                        pb = par * 64
                        k.mm(ps[bx][:, 0:n], iq_t[pb:pb + 64, h // 2, :], ik2[pb:pb + 64, ksl], True, True,
                             [iq_b, ik2b], [psb[bx]])
                        r_, r_b = rt[rri["r"]]
                        rri["r"] = (rri["r"] + 1) % NR
                        rts[h] = (r_, r_b)
                        k.act(r_[:, 0:n], ps[bx][:, 0:n], AF.Relu, [psb[bx]], [r_b])
                    hd = step - LAG
                    if hd >= 0:
                        r_, r_b = rts[hd]
                        k.mm(ps[bsc][:, 0:n], dg[:, hd, :], r_[:, 0:n], hd == 0, hd == 7, [dgb, r_b], [psb[bsc]])
                k.act(score[:, ksl], ps[bsc][:, 0:n], AF.Copy, [psb[bsc]], [scoreb])
            k.tt(POOL, score[:, Lk - 128:Lk], score[:, Lk - 128:Lk], negm, ALU.add, [scoreb] + CB, [scoreb])

        def select(qts):
            ng = len(qts)
            active = [gi for gi, qt in enumerate(qts) if qt * 128 >= topk]
            Las = {}
            for gi in active:
                Lk = qts[gi] * 128 + 128
                score, scoreb = scores[gi]
                Las[gi] = int(Lk * ALPHA) // 64 * 64
                k.reduce(st[:, WD, gi:gi + 1], score[:, 0:Lk], ALU.max, [scoreb], [b_lw])
                k.reduce(st[:, LO, gi:gi + 1], score[:, 0:Lk - 128], ALU.min, [scoreb], [b_lw])
                k.memset(POOL, st[:, THR, gi:gi + 1], float(topk) - 0.5 - Las[gi] / 2.0, [b_thr])
            if active:
                k.tt(DVE, st[:, WD, :], st[:, WD, :], st[:, LO, :], ALU.subtract, [b_lw], [b_lw])
                for it in range(NITER):
                    k.ts(DVE, st[:, WD, :], st[:, WD, :], 0.5, None, ALU.mult, None, [b_lw], [b_lw])
                    k.tt(DVE, st[:, MID, :], st[:, LO, :], st[:, WD, :], ALU.add, [b_lw], [b_mid])
                    for gi in active:
                        Lk = qts[gi] * 128 + 128
                        La = Las[gi]
                        score, scoreb = scores[gi]
                        mask, maskb = masks[gi]
                        mid = st[:, MID, gi:gi + 1]
                        if La > 0:
                          S_.op(ACT, (lambda o_, i_, b_, a_: (lambda e: e.activation(out=o_, in_=i_, func=AF.Sign, scale=-1.0,
                                                                                  bias=b_, accum_out=a_)))(
                              mask[:, 0:La], score[:, 0:La], mid, st[:, CA, gi:gi + 1]), [scoreb, b_mid], [b_mA[gi], b_ca])
                        k.ts(DVE, mask[:, La:Lk], score[:, La:Lk], mid, None, ALU.is_ge, ALU.add, [scoreb, b_mid],
                             [b_mD[gi], b_cnt], accum_out=st[:, CNT, gi:gi + 1])
                    k.stt(DVE, st[:, TOT, :], st[:, CA, :], -0.5, st[:, CNT, :], ALU.mult, ALU.add, [b_ca, b_cnt], [b_cnt])
                    k.tt(DVE, st[:, SW, :], st[:, TOT, :], st[:, THR, :], ALU.is_ge, [b_cnt, b_thr], [b_cnt])
                    k.tt(DVE, st[:, SW, :], st[:, SW, :], st[:, WD, :], ALU.mult, [b_cnt, b_lw], [b_cnt])
                    k.tt(DVE, st[:, LO, :], st[:, LO, :], st[:, SW, :], ALU.add, [b_lw, b_cnt], [b_lw])
            for gi, qt in enumerate(qts):
                Lk = qt * 128 + 128
                score, scoreb = scores[gi]
                mask, maskb = masks[gi]
                if gi in active:
                    thr, thr_r = st[:, LO, gi:gi + 1], [b_lw]
                else:
                    thr, thr_r = eps_t[:, 1:2], CB
                k.ts(DVE, mask[:, 0:Lk], score[:, 0:Lk], thr, None, ALU.is_ge, None, [scoreb] + thr_r,
                     [maskb, b_mA[gi], b_mD[gi]])
                for g0 in range(0, qt + 1, 4):
                    ng_ = min(4, qt + 1 - g0)
                    bt = 6 + rri["t"]
                    rri["t"] ^= 1
                    for j in range(ng_):
                        kt = g0 + j
                        k.mm(ps[bt][:, j * 128:(j + 1) * 128], mask[:, kt * 128:(kt + 1) * 128], identb, True, True,
                             [maskb] + CB, [psb[bt]])
                    m_, m_b = mts[rri["m"]]
                    rri["m"] = (rri["m"] + 1) % 3
                    k.cp(DVE, m_[:, 0:ng_ * 128], ps[bt][:, 0:ng_ * 128], [psb[bt]], [m_b])
                    k.dma(MASKT[qt, :, g0 * 128:(g0 + ng_) * 128], m_[:, 0:ng_ * 128], [m_b], [dram])

        groups = [list(range(g, min(g + G, NQ))) for g in range(0, NQ, G)]
        for qts in groups:
            for gi, qt in enumerate(qts):
                indexer(qt, scores[gi][0], scores[gi][1])
            select(qts)
        S_.barrier()

    def phase_A(l, x_src, x_dst):
        ar.off = base_off
        kt2, kt2b = ar.alloc("kt2", [128, 4, S], BF16)
        vsb, vsbb = ar.alloc("vsb", [128, NQ, 520], BF16)
        woa, woab = ar.alloc("woa", [128, 4, D], BF16)
        wo, wob = ar.alloc("wo", [128, 8, D], BF16)
        mkt, mktb = ar.alloc("mkt", [128, S], BF16)
        wst = [ar.alloc("awst%d" % i, [128, 1024], F32) for i in range(2)]
        qtt = [ar.alloc("qtt%d" % i, [128, 4, 128], BF16) for i in range(2)]
        NPT = 4
        ptl = [ar.alloc("pt%d" % i, [128, 4, 128], BF16) for i in range(NPT)]
        rden, rdenb = ar.alloc("rden", [128, 8], F32)
        on, onb = ar.alloc("on", [128, 512], BF16)
        ogT, ogTb = ar.alloc("ogT", [128, 4, 128], BF16)
        sgt, sgtb = ar.alloc("sgt", [128, 4, 128], BF16)
        mg1t, mg1tb = ar.alloc("mg1t", [128, 8, 128], BF16)
        macct, macctb = ar.alloc("macct", [128, 8, 128], F32)
        xtl, xtlb = ar.alloc("xtl", [128, 8, 128], F32)
        mT, mTb = ar.alloc("mT", [128, 8, 128], BF16)
        mtmp, mtmpb = wst[0][0][:, 0:1024].rearrange("p (c q) -> p c q", c=8), wst[0][1]
        for c in range(4):
            k.dma(kt2[:, c, :], KT[c], [dram], [kt2b])
        vav = VA.rearrange("(t p) f -> p t f", p=128)
        for g0 in range(0, NQ, 8):
            g1 = min(NQ, g0 + 8)
            k.dma(vsb[:, g0:g1, :], vav[:, g0:g1, :], [dram], [vsbb])
        wi = 0
        for kc in range(4):
            w_t, w_b = wst[wi % 2]
            wi += 1
            k.dma(w_t[:], w_oa[l, kc * 128:(kc + 1) * 128, :], [], [w_b])
            k.cp(POOL, woa[:, kc, :], w_t[:], [w_b], [woab])
        for kc in range(8):
            w_t, w_b = wst[wi % 2]
            wi += 1
            k.dma(w_t[:], w_o[l, kc * 128:(kc + 1) * 128, :], [], [w_b])
            k.cp(POOL, wo[:, kc, :], w_t[:], [w_b], [wob])
        rra = {"e": 0, "o": 0, "p": 0}
        OB = (4, 5)

        def tail_stages(qt):
            t0 = qt * 128

            def loads():
                k.dma(sgt[:], SG[:, :, t0:t0 + 128].rearrange("c p t -> p c t"), [dram], [sgtb])
                k.dma(mg1t[:], MG1[:, :, t0:t0 + 128].rearrange("c p t -> p c t"), [dram], [mg1tb])
                k.dma(macct[:], MACC[:, :, t0:t0 + 128].rearrange("c p t -> p c t"), [dram], [macctb])
                k.dma(xtl[:], x_src[:, t0:t0 + 128].rearrange("(c p) t -> p c t", p=128), [dram], [xtlb])

            def st_T():
                bt = 6
                for c in range(4):
                    k.mm(ps[bt][:, c * 128:(c + 1) * 128], on[:, c * 128:(c + 1) * 128], identb, True, True,
                         [onb] + CB, [psb[bt]])
                k.tt(DVE, ogT[:].rearrange("p c q -> p (c q)"), ps[bt][:, :], sgt[:].rearrange("p c q -> p (c q)"),
                     ALU.mult, [psb[bt], sgtb], [ogTb])

            def st_Y():
                for half in range(2):
                    by = 7 if half == 0 else 6
                    for jj in range(4):
                        j = half * 4 + jj
                        for c in range(4):
                            k.mm(ps[by][:, jj * 128:(jj + 1) * 128], woa[:, c, j * 128:(j + 1) * 128], ogT[:, c, :],
                                 c == 0, c == 3, [woab, ogTb], [psb[by]])
                    hs = slice(half * 4, half * 4 + 4)
                    k.tt(DVE, mtmp[:, hs, :].rearrange("p c q -> p (c q)"), ps[by][:, :],
                         mg1t[:, hs, :].rearrange("p c q -> p (c q)"), ALU.mult, [psb[by], mg1tb], [mtmpb])
                k.tt(POOL, mT[:], mtmp[:], macct[:], ALU.add, [mtmpb, macctb], [mTb])

            def st_D():
                for half in range(2):
                    bd = 7 if half == 0 else 6
                    for jj in range(4):
                        jo = half * 4 + jj
                        for j in range(8):
                            k.mm(ps[bd][:, jj * 128:(jj + 1) * 128], wo[:, j, jo * 128:(jo + 1) * 128], mT[:, j, :],
                                 j == 0, j == 7, [wob, mTb], [psb[bd]])
                    hs = slice(half * 4, half * 4 + 4)
                    k.tt(DVE, xtl[:, hs, :].rearrange("p c q -> p (c q)"), ps[bd][:, :],
                         xtl[:, hs, :].rearrange("p c q -> p (c q)"), ALU.add, [psb[bd], xtlb], [xtlb])
                k.dma(x_dst[:, t0:t0 + 128].rearrange("(c p) t -> p c t", p=128), xtl[:], [xtlb], [dram])

            return [loads, st_T, st_Y, st_D]

        pending = []
        for qt in range(NQ):
            t0 = qt * 128
            Lk = t0 + 128
            q_t, q_b = qtt[qt % 2]
            k.dma(q_t[:], QT[:, :, t0:t0 + 128].rearrange("c p t -> p c t"), [dram], [q_b])
            k.dma(mkt[:, 0:Lk], MASKT[qt, :, 0:Lk], [dram], [mktb])
            if pending:
                pending.pop(0)()
            sbank = {}
            ptile = {}

            def emit_S(kt):
                be = rra["e"]
                rra["e"] ^= 1
                bo = 2 + rra["o"]
                rra["o"] ^= 1
                sbank[(kt, 0)] = be
                sbank[(kt, 1)] = bo
                ksl = slice(kt * 128, (kt + 1) * 128)
                for i in range(4):
                    for par, bs in ((0, be), (1, bo)):
                        pb = par * 64
                        k.mm(ps[bs][:, i * 128:(i + 1) * 128], kt2[pb:pb + 64, i, ksl], q_t[pb:pb + 64, i, :], True, True,
                             [kt2b, q_b], [psb[bs]])

            def emit_EM(kt, par):
                bs = sbank[(kt, par)]
                ksl = slice(kt * 128, (kt + 1) * 128)
                p_, p_b = ptl[rra["p"]]
                rra["p"] = (rra["p"] + 1) % NPT
                ptile[(kt, par)] = (p_, p_b)
                k.act(p_[:].rearrange("p h q -> p (h q)"), ps[bs][:, :], AF.Exp, [psb[bs]], [p_b], scale=0.125)
                k.tt(DVE, p_[:], p_[:], mkt[:, ksl].unsqueeze(1).to_broadcast([128, 4, 128]),
                     ALU.mult, [p_b, mktb], [p_b])

            def emit_PV(kt, par):
                p_, p_b = ptile[(kt, par)]
                for i in range(4):
                    h = 2 * i + par
                    k.mm(ps[OB[par]][:, i * 65:(i + 1) * 65], p_[:, i, :], vsb[:, kt, h * 65:(h + 1) * 65],
                         kt == 0 and i == 0, kt == qt, [p_b, vsbb], [psb[OB[par]]], skip=True)

            nk = qt + 1
            emit_S(0)
            for kt in range(nk):
                if kt + 1 < nk:
                    emit_S(kt + 1)
                emit_EM(kt, 0)
                emit_EM(kt, 1)
                emit_PV(kt, 0)
                emit_PV(kt, 1)
                if pending and kt in (0, 1, 2):
                    pending.pop(0)()
            while pending:
                pending.pop(0)()
            for par in range(2):
                ov = ps[OB[par]][:, 0:260].rearrange("p (i e) -> p i e", e=65)
                k.recip(rden[:, par * 4:(par + 1) * 4], ov[:, :, 64], [psb[OB[par]]], [rdenb])
                k.tt(DVE, on[:].rearrange("p (i g d) -> p i g d", g=2, d=64)[:, :, par, :], ov[:, :, 0:64],
                     rden[:, par * 4:(par + 1) * 4].unsqueeze(2).to_broadcast([128, 4, 64]), ALU.mult,
                     [psb[OB[par]], rdenb], [onb])
            pending = tail_stages(qt)
        while pending:
            pending.pop(0)()
        S_.barrier()

    for l in range(L):
        x_src = xT_in if l == 0 else xbuf
        x_dst = outT if l == L - 1 else xbuf
        if "P" in phases:
            phase_P(l, x_src)
        if "I" in phases:
            phase_I(l)
        if "A" in phases:
            phase_A(l, x_src, x_dst)
    S_.barrier()
    S_.emit(nc, stack)
    stack.close()
    return nc, S_


_CACHE = {}


def _get_program(S, L):
    key = (S, L)
    if key not in _CACHE:
        _CACHE[key] = build_program(S, L)[0]
    return _CACHE[key]


def run_layers(x, norm_g, w_in, conv_w, w_out_conv, q_norm_g, k_norm_g, w_out_attn, pool_w, pool_scale,
               w_out_pool, w_o, L=None, nc=None):
    B, S, _ = x.shape
    if L is None:
        L = norm_g.shape[0]
    cosT, sinT, cmat = host_constants(S)
    vecs = host_vectors(norm_g[:L], conv_w[:L], q_norm_g[:L], k_norm_g[:L], pool_scale[:L])
    if nc is None:
        nc = _get_program(S, L)
    shared = {
        "w_in": np.ascontiguousarray(w_in[:L], dtype=np.float32),
        "w_out_conv": np.ascontiguousarray(w_out_conv[:L], dtype=np.float32),
        "w_out_attn": np.ascontiguousarray(w_out_attn[:L], dtype=np.float32),
        "w_out_pool": np.ascontiguousarray(w_out_pool[:L], dtype=np.float32),
        "w_o": np.ascontiguousarray(w_o[:L], dtype=np.float32),
        "pool_w": np.ascontiguousarray(pool_w[:L], dtype=np.float32),
        "vecs": vecs, "cosT": cosT, "sinT": sinT, "cmat": cmat,
    }
    zeros = {k_: np.zeros_like(v_) for k_, v_ in shared.items()}
    in_maps = []
    for c in range(8):
        if c < B:
            m = dict(shared)
            m["xT"] = np.ascontiguousarray(x[c].T, dtype=np.float32)
        else:
            m = dict(zeros)
            m["xT"] = np.zeros((D, S), np.float32)
        in_maps.append(m)
    res = run_bass_kernel_spmd(nc, in_maps, core_ids=list(range(8)))
    out = np.stack([np.ascontiguousarray(res.results[b]["outT"].T) for b in range(B)], axis=0)
    return out.astype(np.float32), res


def kernel(x, norm_g, w_in, conv_w, w_out_conv, q_norm_g, k_norm_g, w_out_attn, pool_w, pool_scale, w_out_pool, w_o):
    args = [np.asarray(a, dtype=np.float32) for a in
            (x, norm_g, w_in, conv_w, w_out_conv, q_norm_g, k_norm_g, w_out_attn, pool_w, pool_scale, w_out_pool, w_o)]
    out, _ = run_layers(*args)
    return out
```
